# Optimizing a Trainium2 kernel written in Bass

```python
import jax, jax.numpy as jnp
from jax import lax
import numpy as np

D_MODEL = 1024
BATCH = 8
SEQ = 2048
DEPTH = 4
DEC_BATCH = 128
DEC_SEQ = 8
PAST_LEN = 16384
PAGE_SIZE = 128

N_MIXERS = 2
N_A_LAYERS = (DEPTH + 1) // 2
N_B_LAYERS = DEPTH // 2
D_RNN = D_MODEL
RG_BLOCKS = 16
RG_BW = D_RNN // RG_BLOCKS
CONV_W = 4
RG_C = 8.0
GLA_HEADS = 4
GLA_DK = D_MODEL // 2
GLA_DV = D_MODEL
GLA_HK = GLA_DK // GLA_HEADS
GLA_HV = GLA_DV // GLA_HEADS
GLA_RANK = 16
GLA_TAU = 16.0
GLA_CHUNK = 64
GLA_IN = 2 * GLA_DK + 2 * GLA_DV + GLA_RANK
N_MEM = 256
XA_HEADS = 4
XA_HD = D_MODEL // XA_HEADS
D_FF = 4 * D_MODEL
EPS = 1e-6

kernel_name = "hybrid_rglru_gla_memxattn_step"


def rmsnorm(x, g):
    xf = x.astype(jnp.float32)
    y = xf * lax.rsqrt(jnp.mean(xf * xf, axis=-1, keepdims=True) + EPS)
    return (y * g.astype(jnp.float32)).astype(x.dtype)


def causal_dwconv(u, buf, w, b):
    T = u.shape[1]
    up = jnp.concatenate([buf.astype(u.dtype), u], axis=1)
    out = b + w[0] * up[:, 0:T]
    for j in range(1, CONV_W):
        out = out + w[j] * up[:, j:j + T]
    return out, up[:, -(CONV_W - 1):]


def rglru(u, h0, w_a, b_a, w_x, b_x, lam):
    B, T, _ = u.shape
    ub = u.reshape(B, T, RG_BLOCKS, RG_BW)
    r = jax.nn.sigmoid(jnp.einsum('bthi,hij->bthj', ub, w_a).reshape(B, T, D_RNN) + b_a)
    i = jax.nn.sigmoid(jnp.einsum('bthi,hij->bthj', ub, w_x).reshape(B, T, D_RNN) + b_x)
    log_a = -RG_C * r.astype(jnp.float32) * jax.nn.softplus(-lam.astype(jnp.float32))
    a = jnp.exp(log_a)
    mult = jnp.sqrt(-jnp.expm1(2.0 * log_a))
    bterm = mult * (i * u).astype(jnp.float32)
    bterm = bterm.at[:, 0].add(a[:, 0] * h0.astype(jnp.float32))

    def combine(left, right):
        a_l, b_l = left
        a_r, b_r = right
        return a_l * a_r, a_r * b_l + b_r

    _, h = lax.associative_scan(combine, (a, bterm), axis=1)
    return h.astype(u.dtype), h[:, -1].astype(h0.dtype)


def rg_block(xn, conv_buf, h0, w_in, conv_w, conv_b, w_a, b_a, w_x, b_x, lam, w_out):
    yx = xn @ w_in
    y_br, x_br = jnp.split(yx, [D_RNN], axis=-1)
    gate = jax.nn.gelu(y_br)
    xc, new_buf = causal_dwconv(x_br, conv_buf, conv_w, conv_b)
    h, h_last = rglru(xc, h0, w_a, b_a, w_x, b_x, lam)
    return (gate * h) @ w_out, new_buf.astype(conv_buf.dtype), h_last


def gla_chunked(q, k, v, log_a, S0, chunk):
    B, T, H, K = q.shape
    V = v.shape[-1]
    n = T // chunk

    def to_chunks(t):
        return jnp.moveaxis(t.reshape(B, n, chunk, H, t.shape[-1]), 1, 0)

    mask = jnp.tril(jnp.ones((chunk, chunk), dtype=bool))

    def step(S, inp):
        qc, kc, vc, lac = inp
        bcum = jnp.cumsum(lac, axis=1)
        g = bcum[:, -1]
        q_in = qc * jnp.exp(bcum)
        k_in = kc * jnp.exp(-bcum)
        att = jnp.where(mask, jnp.einsum('bthk,bshk->bhts', q_in, k_in), 0.0)
        o = jnp.einsum('bhts,bshv->bthv', att, vc) + jnp.einsum('bthk,bhkv->bthv', q_in, S)
        k_end = kc * jnp.exp(g[:, None] - bcum)
        S = jnp.exp(g)[..., None] * S + jnp.einsum('bshk,bshv->bhkv', k_end, vc)
        return S, o

    S, o = lax.scan(step, S0, (to_chunks(q), to_chunks(k), to_chunks(v), to_chunks(log_a)))
    o = jnp.moveaxis(o, 0, 1).reshape(B, T, H, V)
    return o, S


def gla_block(xn, S0, w_in, w_a2, b_a, norm_g, w_out):
    B, T, _ = xn.shape
    proj = xn @ w_in
    q, k, v, g, a_lo = jnp.split(
        proj, [GLA_DK, 2 * GLA_DK, 2 * GLA_DK + GLA_DV, 2 * GLA_DK + 2 * GLA_DV], axis=-1)
    q = q.reshape(B, T, GLA_HEADS, GLA_HK).astype(jnp.float32) * (GLA_HK ** -0.5)
    k = k.reshape(B, T, GLA_HEADS, GLA_HK).astype(jnp.float32)
    v = v.reshape(B, T, GLA_HEADS, GLA_HV).astype(jnp.float32)
    log_a = jax.nn.log_sigmoid((a_lo @ w_a2 + b_a).astype(jnp.float32)) / GLA_TAU
    log_a = log_a.reshape(B, T, GLA_HEADS, GLA_HK)
    chunk = GLA_CHUNK if T % GLA_CHUNK == 0 else T
    o, S = gla_chunked(q, k, v, log_a, S0.astype(jnp.float32), chunk)
    o = rmsnorm(o, norm_g).reshape(B, T, GLA_DV).astype(xn.dtype)
    return (o * jax.nn.silu(g)) @ w_out, S.astype(S0.dtype)


def mem_kv(mem, g, w_k, w_v):
    B, M, _ = mem.shape
    mn = rmsnorm(mem, g)
    return (mn @ w_k).reshape(B, M, XA_HEADS, XA_HD), (mn @ w_v).reshape(B, M, XA_HEADS, XA_HD)


def cross_attn(xn, k, v, w_q, w_o):
    B, T, _ = xn.shape
    q = (xn @ w_q).reshape(B, T, XA_HEADS, XA_HD)
    s = jnp.einsum('bthd,bmhd->bhtm', q, k.astype(q.dtype)).astype(jnp.float32) * (XA_HD ** -0.5)
    p = jax.nn.softmax(s, axis=-1).astype(xn.dtype)
    o = jnp.einsum('bhtm,bmhd->bthd', p, v.astype(xn.dtype)).reshape(B, T, D_MODEL)
    return o @ w_o


def run_group(x, mem_k, mem_v, rg_h, rg_conv, gla_S,
              norm_mix_g, norm_xa_g, norm_mlp_g, final_norm_g,
              rg_w_in, rg_conv_w, rg_conv_b, rg_w_a, rg_b_a, rg_w_x, rg_b_x, rg_lambda, rg_w_out,
              gla_w_in, gla_w_a2, gla_b_a, gla_norm_g, gla_w_out,
              xa_wq, xa_wo, mlp_w1, mlp_w2):
    hs, convs, Ss = [], [], []
    for layer in range(DEPTH):
        j = layer // N_MIXERS
        xn = rmsnorm(x, norm_mix_g[layer])
        if layer % N_MIXERS == 0:
            out, cb, hl = rg_block(xn, rg_conv[j], rg_h[j], rg_w_in[j], rg_conv_w[j], rg_conv_b[j],
                                   rg_w_a[j], rg_b_a[j], rg_w_x[j], rg_b_x[j], rg_lambda[j], rg_w_out[j])
            convs.append(cb)
            hs.append(hl)
        else:
            out, S = gla_block(xn, gla_S[j], gla_w_in[j], gla_w_a2[j], gla_b_a[j], gla_norm_g[j], gla_w_out[j])
            Ss.append(S)
        x = x + out
        x = x + cross_attn(rmsnorm(x, norm_xa_g[layer]), mem_k[layer], mem_v[layer], xa_wq[layer], xa_wo[layer])
        hid = rmsnorm(x, norm_mlp_g[layer]) @ mlp_w1[layer]
        x = x + jnp.square(jax.nn.relu(hid)) @ mlp_w2[layer]
    y = rmsnorm(x, final_norm_g)
    return y, jnp.stack(hs), jnp.stack(convs), jnp.stack(Ss)


def setup_inputs(seed: int = 0) -> dict:
    key = jax.random.key(seed)
    ks = iter(jax.random.split(key, 48))

    def nrm(shape, s):
        return jax.random.normal(next(ks), shape, jnp.float32) * s

    d = D_MODEL
    u = jax.random.uniform(next(ks), (N_A_LAYERS, D_RNN), jnp.float32, minval=0.9, maxval=0.999)
    return {
        "x_prompt": nrm((BATCH, SEQ, d), 1.0),
        "x_sample": nrm((DEC_BATCH, DEC_SEQ, d), 1.0),
        "mem_prompt": nrm((BATCH, N_MEM, d), 1.0),
        "state_rglru_h": nrm((N_A_LAYERS, DEC_BATCH, D_RNN), 0.5),
        "state_rglru_conv": nrm((N_A_LAYERS, DEC_BATCH, CONV_W - 1, D_RNN), 1.0),
        "state_gla_S": nrm((N_B_LAYERS, DEC_BATCH, GLA_HEADS, GLA_HK, GLA_HV), 1.0),
        "cache_mem_k": nrm((DEPTH, DEC_BATCH, N_MEM, XA_HEADS, XA_HD), 1.0),
        "cache_mem_v": nrm((DEPTH, DEC_BATCH, N_MEM, XA_HEADS, XA_HD), 1.0),
        "norm_mix_g": 1.0 + nrm((DEPTH, d), 0.01),
        "norm_xa_g": 1.0 + nrm((DEPTH, d), 0.01),
        "norm_mem_g": 1.0 + nrm((DEPTH, d), 0.01),
        "norm_mlp_g": 1.0 + nrm((DEPTH, d), 0.01),
        "final_norm_g": 1.0 + nrm((d,), 0.01),
        "rg_w_in": nrm((N_A_LAYERS, d, 2 * D_RNN), d ** -0.5),
        "rg_conv_w": nrm((N_A_LAYERS, CONV_W, D_RNN), CONV_W ** -0.5),
        "rg_conv_b": nrm((N_A_LAYERS, D_RNN), 0.01),
        "rg_w_a": nrm((N_A_LAYERS, RG_BLOCKS, RG_BW, RG_BW), RG_BW ** -0.5),
        "rg_b_a": nrm((N_A_LAYERS, D_RNN), 0.01),
        "rg_w_x": nrm((N_A_LAYERS, RG_BLOCKS, RG_BW, RG_BW), RG_BW ** -0.5),
        "rg_b_x": nrm((N_A_LAYERS, D_RNN), 0.01),
        "rg_lambda": jnp.log(u) - jnp.log1p(-u),
        "rg_w_out": nrm((N_A_LAYERS, D_RNN, d), D_RNN ** -0.5),
        "gla_w_in": nrm((N_B_LAYERS, d, GLA_IN), d ** -0.5),
        "gla_w_a2": nrm((N_B_LAYERS, GLA_RANK, GLA_DK), GLA_RANK ** -0.5),
        "gla_b_a": nrm((N_B_LAYERS, GLA_DK), 0.01),
        "gla_norm_g": 1.0 + nrm((N_B_LAYERS, GLA_HV), 0.01),
        "gla_w_out": nrm((N_B_LAYERS, GLA_DV, d), GLA_DV ** -0.5),
        "xa_wq": nrm((DEPTH, d, d), d ** -0.5),
        "xa_wk": nrm((DEPTH, d, d), d ** -0.5),
        "xa_wv": nrm((DEPTH, d, d), d ** -0.5),
        "xa_wo": nrm((DEPTH, d, d), d ** -0.5),
        "mlp_w1": nrm((DEPTH, d, D_FF), d ** -0.5),
        "mlp_w2": nrm((DEPTH, D_FF, d), D_FF ** -0.5),
    }


def reference(x_prompt, x_sample, mem_prompt, state_rglru_h, state_rglru_conv, state_gla_S,
              cache_mem_k, cache_mem_v,
              norm_mix_g, norm_xa_g, norm_mem_g, norm_mlp_g, final_norm_g,
              rg_w_in, rg_conv_w, rg_conv_b, rg_w_a, rg_b_a, rg_w_x, rg_b_x, rg_lambda, rg_w_out,
              gla_w_in, gla_w_a2, gla_b_a, gla_norm_g, gla_w_out,
              xa_wq, xa_wk, xa_wv, xa_wo, mlp_w1, mlp_w2):
    weights = (norm_mix_g, norm_xa_g, norm_mlp_g, final_norm_g,
               rg_w_in, rg_conv_w, rg_conv_b, rg_w_a, rg_b_a, rg_w_x, rg_b_x, rg_lambda, rg_w_out,
               gla_w_in, gla_w_a2, gla_b_a, gla_norm_g, gla_w_out,
               xa_wq, xa_wo, mlp_w1, mlp_w2)

    kv = [mem_kv(mem_prompt, norm_mem_g[l], xa_wk[l], xa_wv[l]) for l in range(DEPTH)]
    mem_k_prompt = jnp.stack([kv_l[0] for kv_l in kv])
    mem_v_prompt = jnp.stack([kv_l[1] for kv_l in kv])
    dt = x_prompt.dtype
    h0_p = jnp.zeros((N_A_LAYERS, BATCH, D_RNN), dt)
    conv0_p = jnp.zeros((N_A_LAYERS, BATCH, CONV_W - 1, D_RNN), dt)
    S0_p = jnp.zeros((N_B_LAYERS, BATCH, GLA_HEADS, GLA_HK, GLA_HV), dt)
    y_prompt, rglru_h_prompt, rglru_conv_prompt, gla_S_prompt = run_group(
        x_prompt, mem_k_prompt, mem_v_prompt, h0_p, conv0_p, S0_p, *weights)

    y_sample, rglru_h_sample, rglru_conv_sample, gla_S_sample = run_group(
        x_sample, cache_mem_k, cache_mem_v, state_rglru_h, state_rglru_conv, state_gla_S, *weights)

    return (y_prompt, y_sample, mem_k_prompt, mem_v_prompt, rglru_h_prompt, rglru_conv_prompt,
            gla_S_prompt, rglru_h_sample, rglru_conv_sample, gla_S_sample)
```

```python
import math
from contextlib import ExitStack

import numpy as np
import concourse.bass as bass
import concourse.mybir as mybir
from concourse.bass_utils import run_bass_kernel_spmd

F32 = mybir.dt.float32
BF16 = mybir.dt.bfloat16
ALU = mybir.AluOpType
AF = mybir.ActivationFunctionType

NCORES = 8
D = 1024
NCH = 8
SEQ = 2048
NSEQ_S = 16
LS = 8
NT = SEQ + NSEQ_S * LS
TILES = [(0, 512), (512, 512), (1024, 512), (1536, 512), (2048, 128)]
DEPTH = 4
EPS = 1e-6
GLA_IN = 3088
NSLOT = 3
N_TF = 8
N_TB = 6
SEM_WRAP = 30000
ENGS = ("pe", "act", "dve", "pool", "sp")


class Op:
    __slots__ = ("eng", "fn", "reads", "writes", "dma", "idx", "deps", "signal",
                 "sem", "val", "prewait")

    def __init__(self, eng, fn, reads, writes, dma):
        self.eng = eng
        self.fn = fn
        self.reads = reads
        self.writes = writes
        self.dma = dma
        self.deps = []
        self.signal = False
        self.sem = None
        self.val = 0
        self.prewait = None


class Prog:
    def __init__(self, nc, n_dma_sems=16):
        self.nc = nc
        self.ops = []
        self.n_dma_sems = n_dma_sems

    def add(self, eng, fn, reads=(), writes=(), dma=False):
        op = Op(eng, fn, tuple(reads), tuple(writes), dma)
        op.idx = len(self.ops)
        self.ops.append(op)
        return op

    def _analyze(self):
        last_w = {}
        readers = {}
        ops = self.ops
        al = getattr(self, "alias", {})
        if al:
            for op in ops:
                if any(c in al for c in op.reads):
                    op.reads = tuple(x for c in op.reads for x in al.get(c, (c,)))
                if any(c in al for c in op.writes):
                    op.writes = tuple(x for c in op.writes for x in al.get(c, (c,)))
        for op in ops:
            deps = set()
            for c in op.reads:
                w = last_w.get(c)
                if w is not None:
                    deps.add(w)
            for c in op.writes:
                w = last_w.get(c)
                if w is not None:
                    deps.add(w)
                for r in readers.get(c, ()):
                    deps.add(r)
            deps.discard(op.idx)
            for c in op.reads:
                readers.setdefault(c, []).append(op.idx)
            for c in op.writes:
                last_w[c] = op.idx
                readers[c] = []
            dl = []
            for d in deps:
                dop = ops[d]
                if (not dop.dma) and (not op.dma) and dop.eng == "pe" and op.eng == "pe":
                    continue
                dl.append(d)
            op.deps = dl
            for d in dl:
                ops[d].signal = True

    def emit(self, stack):
        nc = self.nc
        self._analyze()
        cnt = {e: 0 for e in ENGS}
        eng_sems = {e: [] for e in ENGS}
        dma_sems = {e: [] for e in ENGS}
        dma_rr = {e: 0 for e in ENGS}
        dma_tot = {}
        for op in self.ops:
            if op.dma:
                lst = dma_sems[op.eng]
                k = dma_rr[op.eng] % self.n_dma_sems
                dma_rr[op.eng] += 1
                if k >= len(lst):
                    lst.append(stack.enter_context(nc.semaphore("d_%s_%d" % (op.eng, k))))
                s = lst[k]
                prev = dma_tot.get(s, 0)
                op.prewait = (s, prev) if prev > 0 else None
                dma_tot[s] = prev + 16
                op.sem = s
                op.val = prev + 16
            elif op.signal:
                c = cnt[op.eng]
                k = c // SEM_WRAP
                lst = eng_sems[op.eng]
                while k >= len(lst):
                    lst.append(stack.enter_context(nc.semaphore("e_%s_%d" % (op.eng, len(lst)))))
                op.sem = lst[k]
                op.val = c - k * SEM_WRAP + 1
                cnt[op.eng] = c + 1
        per_eng = {e: [o for o in self.ops if o.eng == e] for e in ENGS}
        ops = self.ops
        all_dma = list(dma_tot.items())

        def run(eng_name, eng):
            waited = {}
            for op in per_eng[eng_name]:
                need = {}
                if op.prewait is not None:
                    s, v = op.prewait
                    need[s] = max(need.get(s, 0), v)
                for d in op.deps:
                    dop = ops[d]
                    s, v = dop.sem, dop.val
                    if need.get(s, 0) < v:
                        need[s] = v
                for s, v in need.items():
                    if waited.get(s, 0) >= v:
                        continue
                    eng.wait_ge(s, v)
                    waited[s] = v
                ins = op.fn(eng)
                if op.dma:
                    ins.then_inc(op.sem, 16)
                elif op.signal:
                    ins.then_inc(op.sem, 1)
            if eng_name == "sp":
                for s, v in all_dma:
                    if waited.get(s, 0) < v:
                        eng.wait_ge(s, v)

        block = stack.enter_context(nc.Block())

        @block.tensor
        def _(e):
            run("pe", e)

        @block.scalar
        def _(e):
            run("act", e)

        @block.vector
        def _(e):
            run("dve", e)

        @block.gpsimd
        def _(e):
            run("pool", e)

        @block.sync
        def _(e):
            run("sp", e)


class FreeList:
    def __init__(self, items, name):
        self.free = list(items)
        self.name = name

    def get(self):
        assert self.free, "pool %s exhausted" % self.name
        return self.free.pop(0)

    def put(self, it):
        self.free.append(it)


def build_program(stage=None):
    nc = bass.Bass("TRN2", target_bir_lowering=False)
    if stage is None:
        stage = 99

    def enabled(kind, l=0):
        if kind == "memkv":
            return stage >= 1
        base = 2 + 3 * l
        return stage >= base + {"mix": 0, "xa": 1, "mlp": 2}[kind]

    def din(name, shape):
        return nc.dram_tensor(name, list(shape), F32, kind="ExternalInput").ap()

    def dout(name, shape):
        return nc.dram_tensor(name, list(shape), F32, kind="ExternalOutput").ap()

    xp = din("xp", [SEQ, D])
    xs = din("xs", [128, D])
    mem = din("mem", [256, D])
    st_h = din("st_h", [2, 16, D])
    st_conv = din("st_conv", [2, 48, D])
    st_S = din("st_S", [2, 16, 4, 128, 256])
    ck = din("ck", [DEPTH, 16, 256, D])
    cv = din("cv", [DEPTH, 16, 256, D])
    pvec = din("pvec", [128, 320])
    consts = din("consts", [128, 1168])
    rg_w_in = din("rg_w_in", [2, D, 2048])
    rg_w_a = din("rg_w_a", [2, 16, 64, 64])
    rg_w_x = din("rg_w_x", [2, 16, 64, 64])
    rg_w_out = din("rg_w_out", [2, D, D])
    gla_w_in = din("gla_w_in", [2, D, GLA_IN])
    gla_w_a2 = din("gla_w_a2", [2, 16, 512])
    gla_w_out = din("gla_w_out", [2, D, D])
    xa_wq = din("xa_wq", [DEPTH, D, D])
    xa_wk = din("xa_wk", [DEPTH, D, D])
    xa_wv = din("xa_wv", [DEPTH, D, D])
    xa_wo = din("xa_wo", [DEPTH, D, D])
    mlp_w1 = din("mlp_w1", [DEPTH, D, 4096])
    mlp_w2 = din("mlp_w2", [DEPTH, 4096, D])

    o_yp = dout("o_yp", [SEQ, D])
    o_ys = dout("o_ys", [128, D])
    o_mk = dout("o_mk", [DEPTH, 256, D])
    o_mv = dout("o_mv", [DEPTH, 256, D])
    o_hp = dout("o_hp", [2, D])
    o_cp = dout("o_cp", [2, 3, D])
    o_Sp = dout("o_Sp", [2, 4, 128, 256])
    o_hs = dout("o_hs", [2, 16, D])
    o_cs = dout("o_cs", [2, 48, D])
    o_Ss = dout("o_Ss", [2, 16, 4, 128, 256])

    st = ExitStack()
    with st:
        P = Prog(nc)

        def sb(name, shape, dt):
            return st.enter_context(nc.sbuf_tensor(name, list(shape), dt))

        xT = sb("xT", [128, NCH, NT], F32)
        xnT = sb("xnT", [128, NCH, NT], BF16)
        actT = sb("actT", [128, NCH, NT], BF16)
        slots = FreeList([(sb("wslot%d" % i, [128, 4096], BF16), "w%d" % i) for i in range(NSLOT)], "slots")
        tfbig = sb("tfbig", [128, N_TF * 516], F32)
        tf_all = [(tfbig[:, i * 516:(i + 1) * 516], "tf%d" % i) for i in range(N_TF)]
        tfp = FreeList(list(reversed(tf_all)), "tf")
        slot_x = (tfbig[:, 0:2048].bitcast(BF16), "wx")
        P.alias = {"wx": ["tf0", "tf1", "tf2", "tf3"]}

        def use_slot_x(on):
            if on:
                for it_ in tf_all[0:4]:
                    tfp.free.remove(it_)
                slots.free.append(slot_x)
            else:
                slots.free.remove(slot_x)
                for it_ in tf_all[0:4]:
                    tfp.free.append(it_)
        tbp = FreeList([(sb("tb%d" % i, [128, 1024], BF16), "tb%d" % i) for i in range(N_TB)], "tb")
        banks = FreeList([(st.enter_context(nc.psum_tensor("ps%d" % i, [128, 512], F32)), "ps%d" % i)
                          for i in range(8)], "psum")
        pv = sb("pv", [128, 276], F32)
        dv = sb("dv", [128, 96], F32)
        identf = sb("identf", [128, 128], F32)
        identb = sb("identb", [128, 128], BF16)
        amask = sb("amask", [128, 256], BF16)
        cmask = sb("cmask", [128, 640], F32)
        chs = sb("chs", [128, 16], F32)
        ones_m = sb("ones_m", [128, 384], BF16)
        cst = sb("cst", [128, 4], F32)
        wa2 = sb("wa2", [16, 2, 512], BF16)
        carry = sb("carry", [128, NCH, 4], F32)
        st0 = sb("st0", [128, NCH, 64], F32)
        stout = sb("stout", [128, NCH, 68], F32)
        Sf_p = sb("Sf_p", [128, 256], F32)
        Sb_p = sb("Sb_p", [128, 256], BF16)
        Sfs = FreeList([(sb("Sfs%d" % i, [128, 256], F32), "Sfs%d" % i) for i in range(4)], "Sfs")
        Sbs = FreeList([(sb("Sbs%d" % i, [128, 256], BF16), "Sbs%d" % i) for i in range(3)], "Sbs")
        small = sb("small", [128, 64], F32)

        def ACT(out, in_, func, reads, writes, scale=None, bias=None):
            kw = {}
            if scale is not None:
                kw["scale"] = scale
            if bias is not None:
                kw["bias"] = bias
            P.add("act", lambda e: e.activation(out=out, in_=in_, func=func, **kw), reads, writes)

        def TT(out, in0, in1, op, reads, writes, eng="dve"):
            P.add(eng, lambda e: e.tensor_tensor(out=out, in0=in0, in1=in1, op=op), reads, writes)

        def TS(out, in0, s1, s2, op0, op1, reads, writes, eng="dve"):
            P.add(eng, lambda e: e.tensor_scalar(out=out, in0=in0, scalar1=s1, scalar2=s2, op0=op0, op1=op1),
                  reads, writes)

        def STT(out, in0, scalar, in1, op0, op1, reads, writes, eng="dve"):
            P.add(eng, lambda e: e.scalar_tensor_tensor(out=out, in0=in0, scalar=scalar, in1=in1,
                                                         op0=op0, op1=op1), reads, writes)

        def CP(out, in_, reads, writes, eng="dve"):
            P.add(eng, lambda e: e.tensor_copy(out=out, in_=in_), reads, writes)

        def ACP(out, in_, reads, writes):
            ACT(out, in_, AF.Copy, reads, writes)

        def MM(mms, reads, writes):
            def fn(e):
                ins = None
                for (o, l, r, s1, s2) in mms:
                    ins = e.matmul(o, lhsT=l, rhs=r, start=s1, stop=s2)
                return ins
            P.add("pe", fn, reads, writes)

        def TR(items, reads, writes):
            def fn(e):
                ins = None
                for (o, i, idn) in items:
                    ins = e.transpose(out=o, in_=i, identity=idn)
                return ins
            P.add("pe", fn, reads, writes)

        def DMA(q, out, in_, reads, writes):
            P.add(q, lambda e: e.dma_start(out=out, in_=in_), reads, writes, dma=True)

        def MEMSET(ap, val, writes, eng="pool"):
            P.add(eng, lambda e: e.memset(ap, val), (), writes)

        def run_lanes(gens, nlanes, lag):
            queue = list(gens)
            lanes = [None] * nlanes
            delay = [i * lag for i in range(nlanes)]
            while queue or any(g is not None for g in lanes):
                for i in range(nlanes):
                    if delay[i] > 0:
                        delay[i] -= 1
                        continue
                    if lanes[i] is None:
                        if not queue:
                            continue
                        lanes[i] = queue.pop(0)
                    try:
                        next(lanes[i])
                    except StopIteration:
                        lanes[i] = None

        def xcell(c, t):
            return "x%d_%d" % (c, t)

        def ncell(c, t):
            return "n%d_%d" % (c, t)

        def acell(c, t):
            return "a%d_%d" % (c, t)

        ALLN = lambda t: [ncell(c, t) for c in range(NCH)]
        ALLA = lambda t: [acell(c, t) for c in range(NCH)]

        class WS:
            def __init__(self):
                self.plan = []
                self.issued = 0
                self.consumed = 0
                self.loaded = {}
                self.ahead = 2

            def _issue(self, want=None):
                while (self.issued < len(self.plan) and slots.free
                       and self.issued - self.consumed < self.ahead):
                    tag, fn = self.plan[self.issued]
                    if tag.startswith("kvp") and tag != want:
                        break
                    okx = tag.startswith(("kvs", "wo"))
                    cand = [x for x in slots.free if okx or x[1] != "wx"]
                    if not cand:
                        break
                    slot = cand[0]
                    slots.free.remove(slot)
                    for (o, i, rd) in fn(slot[0]):
                        if isinstance(i, float):
                            MEMSET(o, i, [slot[1]])
                        else:
                            DMA("pool", o, i, rd, [slot[1]])
                    self.loaded[self.issued] = slot
                    self.issued += 1

            def next(self, tag):
                self._issue(tag)
                assert self.consumed < self.issued, "weight stream starved at %s" % tag
                assert self.plan[self.consumed][0] == tag, (self.plan[self.consumed][0], tag)
                slot = self.loaded.pop(self.consumed)
                self.consumed += 1
                self._issue()
                return slot

            def release(self, slot):
                slots.put(slot)

        ws = WS()

        def piece_cols(W, col0, ncols, dst0=0):
            def fn(slot):
                o = slot[:, dst0:dst0 + 8 * ncols].rearrange("p (k n) -> p k n", n=ncols)
                i = W[:, col0:col0 + ncols].rearrange("(k p) n -> p k n", p=128)
                return [(o, i, [])]
            return fn

        def multi(*fns):
            def fn(slot):
                r = []
                for f in fns:
                    r += f(slot)
                return r
            return fn

        def build_plan():
            plan = []
            for l in range(DEPTH):
                if enabled("memkv"):
                    for hf in range(2):
                        plan.append(("wk%d_%d" % (l, hf), piece_cols(xa_wk[l], hf * 512, 512)))
                    for hf in range(2):
                        plan.append(("wv%d_%d" % (l, hf), piece_cols(xa_wv[l], hf * 512, 512)))
            for l in range(DEPTH):
                j = l // 2
                if not enabled("mix", l):
                    pass
                elif l % 2 == 0:
                    for c in range(NCH):
                        def gates(c=c, j=j):
                            def fn(slot):
                                r = [(slot[:, 2048:2304], 0.0, [])]
                                for g, W in ((0, rg_w_a), (1, rg_w_x)):
                                    for hb in range(2):
                                        r.append((slot[hb * 64:(hb + 1) * 64,
                                                       2048 + g * 128 + hb * 64:2048 + g * 128 + (hb + 1) * 64],
                                                  W[j, 2 * c + hb], []))
                                return r
                            return fn
                        plan.append(("rgin%d_%d" % (l, c), multi(
                            piece_cols(rg_w_in[j], c * 128, 128, 0),
                            piece_cols(rg_w_in[j], 1024 + c * 128, 128, 1024),
                            gates())))
                    for hf in range(2):
                        plan.append(("mixout%d_%d" % (l, hf), piece_cols(rg_w_out[j], hf * 512, 512)))
                else:
                    plan.append(("glalo%d" % l, piece_cols(gla_w_in[j], 3072, 16)))
                    for h in range(4):
                        plan.append(("glaA%d_%d" % (l, h), multi(
                            piece_cols(gla_w_in[j], h * 128, 128, 0),
                            piece_cols(gla_w_in[j], 512 + h * 128, 128, 1024),
                            piece_cols(gla_w_in[j], 1024 + h * 256, 256, 2048))))
                        plan.append(("glaB%d_%d" % (l, h), piece_cols(gla_w_in[j], 2048 + h * 256, 256)))
                    for hf in range(2):
                        plan.append(("mixout%d_%d" % (l, hf), piece_cols(gla_w_out[j], hf * 512, 512)))

                def kvp(l=l):
                    def fn(slot):
                        ok = slot[:, 0:2048].rearrange("p (b n) -> p b n", n=1024)
                        ov = slot[:, 2048:4096].rearrange("p (b n) -> p b n", n=1024)
                        return [(ok, o_mk[l].rearrange("(b p) n -> p b n", p=128), ["memk%d" % l]),
                                (ov, o_mv[l].rearrange("(b p) n -> p b n", p=128), ["memv%d" % l])]
                    return fn
                if enabled("xa", l):
                    plan.append(("kvp%d" % l, kvp()))
                    for h in range(4):
                        plan.append(("wq%d_%d" % (l, h), piece_cols(xa_wq[l], h * 256, 256)))

                def kvs(s, l=l):
                    def fn(slot):
                        ok = slot[:, 0:2048].rearrange("p (b n) -> p b n", n=1024)
                        ov = slot[:, 2048:4096].rearrange("p (b n) -> p b n", n=1024)
                        return [(ok, ck[l, s].rearrange("(b p) n -> p b n", p=128), []),
                                (ov, cv[l, s].rearrange("(b p) n -> p b n", p=128), [])]
                    return fn
                for s in range(NSEQ_S):
                    if enabled("xa", l):
                        plan.append(("kvs%d_%d" % (l, s), kvs(s)))
                        if s == 1:
                            plan.append(("wo%d_0" % l, piece_cols(xa_wo[l], 0, 512)))
                        if s == 9:
                            plan.append(("wo%d_1" % l, piece_cols(xa_wo[l], 512, 512)))
                if enabled("xa", l):
                    plan.append(("wo%d_0b" % l, piece_cols(xa_wo[l], 0, 512)))
                for g in range(8):
                    if not enabled("mlp", l):
                        break
                    plan.append(("w1_%d_%d" % (l, g), piece_cols(mlp_w1[l], g * 512, 512)))

                    def w2p(g=g, l=l):
                        def fn(slot):
                            o = slot[:, 0:4096].rearrange("p (k n) -> p k n", n=1024)
                            i = mlp_w2[l, g * 512:(g + 1) * 512, :].rearrange("(k p) n -> p k n", p=128)
                            return [(o, i, [])]
                        return fn
                    plan.append(("w2_%d_%d" % (l, g), w2p()))
            return plan

        ws.plan = build_plan()

        DMA("sp", pv[:], pvec[:, 0:276], [], ["pv"])
        DMA("sp", identf[:], consts[:, 0:128], [], ["identf"])
        DMA("sp", cmask[:], consts[:, 384:1024], [], ["cmask"])
        DMA("sp", chs[:], consts[:, 1024:1040], [], ["chs"])
        DMA("pool", identb[:], consts[:, 0:128], [], ["identb"])
        DMA("pool", amask[:], consts[:, 128:384], [], ["amask"])
        DMA("pool", wa2[:], gla_w_a2.rearrange("j r n -> r j n"), [], ["wa2"])
        MEMSET(ones_m[:, 0:128], 1.0 / 1024.0, ["ones_m"])
        MEMSET(ones_m[:, 128:256], 1.0 / 256.0, ["ones_m"])
        MEMSET(ones_m[:, 256:384], 1.0, ["ones_m"])
        MEMSET(cst[:, 0:1], EPS, ["cst"])
        MEMSET(cst[:, 1:2], 1.0, ["cst"])
        MEMSET(cst[:, 2:3], math.log(0.5), ["cst"])
        MEMSET(cst[:, 3:4], 0.0, ["cst"])
        c_eps = cst[:, 0:1]
        c_one = cst[:, 1:2]
        c_lnh = cst[:, 2:3]
        ones1024 = ones_m[:, 0:128]
        ones256 = ones_m[:, 128:256]
        ones1 = ones_m[:, 256:384]

        PV_MIX, PV_XA, PV_MEM, PV_MLP, PV_FIN = 0, 32, 64, 96, 128
        PV_CW, PV_CB, PV_BA, PV_BX, PV_LAM = 136, 200, 216, 232, 248
        PV_GBA, PV_GNG = 264, 272
        TS(dv[:, 0:16], pv[:, PV_BA:PV_BA + 16], 0.5, 0.0, ALU.mult, ALU.add, ["pv"], ["dv"])
        TS(dv[:, 16:32], pv[:, PV_BX:PV_BX + 16], 0.5, 0.0, ALU.mult, ALU.add, ["pv"], ["dv"])
        ACT(dv[:, 32:48], pv[:, PV_LAM:PV_LAM + 16], AF.Exp, ["pv"], ["dv"], scale=-1.0)
        ACT(dv[:, 32:48], dv[:, 32:48], AF.Ln, ["dv", "cst"], ["dv"], bias=c_one)
        TS(dv[:, 48:64], dv[:, 32:48], -8.0, 0.0, ALU.mult, ALU.add, ["dv"], ["dv"])
        TS(dv[:, 32:48], dv[:, 32:48], -4.0, 0.0, ALU.mult, ALU.add, ["dv"], ["dv"])
        TS(dv[:, 64:72], pv[:, PV_GBA:PV_GBA + 8], -1.0, 0.0, ALU.mult, ALU.add, ["pv"], ["dv"])
        TS(dv[:, 72:76], pv[:, PV_GNG:PV_GNG + 4], 0.5, 0.0, ALU.mult, ALU.add, ["pv"], ["dv"])

        def tcols(t):
            lo, n = TILES[t]
            return lo, n

        def load_rows_T(src_rows, tok0, t):
            for hf in range(2):
                tfb = tfp.get()
                DMA("sp", tfb[0][:, 0:512], src_rows[:, hf * 512:(hf + 1) * 512], [], [tfb[1]])
                bk = banks.get()
                TR([(bk[0][:, q * 128:(q + 1) * 128], tfb[0][:, q * 128:(q + 1) * 128], identf[:])
                    for q in range(4)], [tfb[1], "identf"], [bk[1]])
                tfp.put(tfb)
                ACP(xT[:, hf * 4:hf * 4 + 4, tok0:tok0 + 128],
                    bk[0][:, 0:512].rearrange("p (q n) -> p q n", n=128),
                    [bk[1]], [xcell(c, t) for c in range(hf * 4, hf * 4 + 4)])
                banks.put(bk)

        for i in range(16):
            load_rows_T(xp[i * 128:(i + 1) * 128, :], i * 128, i // 4)
        load_rows_T(xs[:, :], 2048, 4)

        def norm_rstd(t):
            lo, n = tcols(t)
            bk = banks.get()
            for c in range(NCH):
                sq = tbp.get()
                ACT(sq[0][:, 0:n], xT[:, c, lo:lo + n], AF.Square, [xcell(c, t)], [sq[1]])
                MM([(bk[0][:, 0:n], ones1024, sq[0][:, 0:n], c == 0, c == NCH - 1)],
                   [sq[1], "ones_m"], [bk[1]])
                tbp.put(sq)
            rs = tfp.get()
            ACT(rs[0][:, 0:n], bk[0][:, 0:n], AF.Ln, [bk[1], "cst"], [rs[1]], bias=c_eps)
            banks.put(bk)
            ACT(rs[0][:, 0:n], rs[0][:, 0:n], AF.Exp, [rs[1]], [rs[1]], scale=-0.5)
            return rs

        def norm_to_xn(gcol):
            for t in range(len(TILES)):
                lo, n = tcols(t)
                rs = norm_rstd(t)
                for c in range(NCH):
                    STT(xnT[:, c, lo:lo + n], xT[:, c, lo:lo + n], pv[:, gcol + c:gcol + c + 1],
                        rs[0][:, 0:n], ALU.mult, ALU.mult, [xcell(c, t), rs[1], "pv"], [ncell(c, t)])
                tfp.put(rs)

        def out_proj(tags, src, src_cells):
            for hf in range(2):
                slot = ws.next(tags[hf])
                wv_ = slot[0][:, 0:4096].rearrange("p (k n) -> p k n", n=512)
                for oc4 in range(4):
                    oc = hf * 4 + oc4
                    for t in range(len(TILES)):
                        lo, n = tcols(t)
                        bk = banks.get()
                        MM([(bk[0][:, 0:n], wv_[:, k, oc4 * 128:(oc4 + 1) * 128], src[:, k, lo:lo + n],
                             k == 0, k == NCH - 1) for k in range(NCH)],
                           [slot[1]] + src_cells(t), [bk[1]])
                        TT(xT[:, oc, lo:lo + n], xT[:, oc, lo:lo + n], bk[0][:, 0:n], ALU.add,
                           [bk[1], xcell(oc, t)], [xcell(oc, t)])
                        banks.put(bk)
                ws.release(slot)

        def mem_kv():
            rstd_m = small[:, 0:2]
            for b in range(2):
                for hf in range(2):
                    tfb = tfp.get()
                    DMA("sp", tfb[0][:, 0:512], mem[b * 128:(b + 1) * 128, hf * 512:(hf + 1) * 512], [], [tfb[1]])
                    junk = tfp.get()
                    ACT(junk[0][:, 0:512], tfb[0][:, 0:512], AF.Square, [tfb[1]], [junk[1], "small"],)
                    P.add("dve", (lambda o, i: (lambda e: e.reduce_sum(out=o, in_=i, axis=mybir.AxisListType.X)))(
                        small[:, 4 + b * 2 + hf:5 + b * 2 + hf], junk[0][:, 0:512]), [junk[1]], ["small"])
                    tfp.put(junk)
                    tfp.put(tfb)
                TT(small[:, 8 + b:9 + b], small[:, 4 + b * 2:5 + b * 2], small[:, 5 + b * 2:6 + b * 2], ALU.add,
                   ["small"], ["small"])
            TS(small[:, 0:2], small[:, 8:10], 1.0 / 1024.0, 0.0, ALU.mult, ALU.add, ["small"], ["small"])
            ACT(small[:, 0:2], small[:, 0:2], AF.Ln, ["small", "cst"], ["small"], bias=c_eps)
            ACT(small[:, 0:2], small[:, 0:2], AF.Exp, ["small"], ["small"], scale=-0.5)
        def mem_kv_layer(l):
            rstd_m = small[:, 0:2]
            if True:
                mn = [tbp.get(), tbp.get()]
                for b in range(2):
                    for hf in range(2):
                        tfb = tfp.get()
                        DMA("act", tfb[0][:, 0:512], mem[b * 128:(b + 1) * 128, hf * 512:(hf + 1) * 512], [], [tfb[1]])
                        TS(tfb[0][:, 0:512], tfb[0][:, 0:512], rstd_m[:, b:b + 1], 0.0, ALU.mult, ALU.add,
                           [tfb[1], "small"], [tfb[1]])
                        bk = banks.get()
                        TR([(bk[0][:, q * 128:(q + 1) * 128], tfb[0][:, q * 128:(q + 1) * 128], identf[:])
                            for q in range(4)], [tfb[1], "identf"], [bk[1]])
                        tfp.put(tfb)
                        for q in range(4):
                            c = hf * 4 + q
                            mview = mn[hf][0][:, q * 256 + b * 128:q * 256 + (b + 1) * 128]
                            ACT(mview, bk[0][:, q * 128:(q + 1) * 128], AF.Copy, [bk[1], "pv"], [mn[hf][1]],
                                scale=pv[:, PV_MEM + l * 8 + c:PV_MEM + l * 8 + c + 1])
                        banks.put(bk)

                def mnT(k, b):
                    return mn[k // 4][0][:, (k % 4) * 256 + b * 128:(k % 4) * 256 + (b + 1) * 128]
                for (nm, dst, cellp) in (("wk", o_mk, "memk"), ("wv", o_mv, "memv")):
                    for hf in range(2):
                        slot = ws.next("%s%d_%d" % (nm, l, hf))
                        wv_ = slot[0][:, 0:4096].rearrange("p (k n) -> p k n", n=512)
                        for b in range(2):
                            bk = banks.get()
                            MM([(bk[0][:, 0:512], mnT(k, b), wv_[:, k, :], k == 0, k == NCH - 1)
                                for k in range(NCH)], [slot[1], mn[0][1], mn[1][1]], [bk[1]])
                            tfb = tfp.get()
                            ACP(tfb[0][:, 0:512], bk[0][:, 0:512], [bk[1]], [tfb[1]])
                            banks.put(bk)
                            DMA("sp", dst[l, b * 128:(b + 1) * 128, hf * 512:(hf + 1) * 512], tfb[0][:, 0:512],
                                [tfb[1]], ["%s%d" % (cellp, l)])
                            tfp.put(tfb)
                        ws.release(slot)
                tbp.put(mn[0])
                tbp.put(mn[1])

        if enabled("memkv"):
            mem_kv()
            for l in range(DEPTH):
                mem_kv_layer(l)

        C_GELU = math.sqrt(2.0 / math.pi)

        def rg_block(l):
            j = l // 2
            for hf in range(2):
                tfb = tfp.get()
                DMA("sp", tfb[0][0:48, 0:512], st_conv[j, :, hf * 512:(hf + 1) * 512], [], [tfb[1]])
                DMA("sp", tfb[0][48:64, 0:512], st_h[j, :, hf * 512:(hf + 1) * 512], [], [tfb[1]])
                bk = banks.get()
                TR([(bk[0][:, q * 64:(q + 1) * 64], tfb[0][0:64, q * 128:(q + 1) * 128], identf[0:64, 0:64])
                    for q in range(4)], [tfb[1], "identf"], [bk[1]])
                tfp.put(tfb)
                ACP(st0[:, hf * 4:hf * 4 + 4, :], bk[0][:, 0:256].rearrange("p (q n) -> p q n", n=64),
                    [bk[1]], ["st0"])
                banks.put(bk)
            norm_to_xn(PV_MIX + l * 8)
            held = {}

            ulist = [(c, t) for c in range(NCH) for t in range(len(TILES))]
            projd = {}
            hlast = {}

            def ensure_proj(ui):
                if ui >= len(ulist) or ui in projd:
                    return
                c, t = ulist[ui]
                if t == 0:
                    held[c] = ws.next("rgin%d_%d" % (l, c))
                slot = held[c]
                wy = slot[0][:, 0:1024].rearrange("p (k n) -> p k n", n=128)
                wx = slot[0][:, 1024:2048].rearrange("p (k n) -> p k n", n=128)
                lo, n = tcols(t)
                bky = banks.get()
                bkx = banks.get()
                MM([(bky[0][:, 0:n], wy[:, k, :], xnT[:, k, lo:lo + n], k == 0, k == NCH - 1)
                    for k in range(NCH)], [slot[1]] + ALLN(t), [bky[1]])
                MM([(bkx[0][:, 0:n], wx[:, k, :], xnT[:, k, lo:lo + n], k == 0, k == NCH - 1)
                    for k in range(NCH)], [slot[1]] + ALLN(t), [bkx[1]])
                projd[ui] = (bky, bkx)

            def unit(ui):
                c, t = ulist[ui]
                ensure_proj(ui)
                bky, bkx = projd.pop(ui)
                slot = held[c]
                gwa = slot[0][:, 2048:2176]
                gwx = slot[0][:, 2176:2304]
                pcol = j * 8 + c
                lo, n = tcols(t)
                samp = (t == 4)
                nseg, L = (16, 8) if samp else (1, n)
                W_ = nseg * (L + 3)
                gt = tfp.get()
                ACT(gt[0][:, 0:n], bky[0][:, 0:n], AF.Square, [bky[1]], [gt[1]], scale=math.sqrt(0.044715))
                xbr = tfp.get()
                xb3 = xbr[0][:, 0:W_].rearrange("p (s w) -> p s w", w=L + 3)
                CP(xb3[:, :, 3:3 + L], bkx[0][:, 0:n].rearrange("p (s w) -> p s w", w=L),
                   [bkx[1]], [xbr[1]])
                banks.put(bkx)
                if samp:
                    CP(xb3[:, :, 0:3], st0[:, c, 0:48].rearrange("p (s w) -> p s w", w=3),
                       ["st0"], [xbr[1]], eng="pool")
                elif t == 0:
                    MEMSET(xbr[0][:, 0:3], 0.0, [xbr[1]])
                else:
                    CP(xbr[0][:, 0:3], carry[:, c, 0:3], ["carry%d" % c], [xbr[1]], eng="pool")
                if not samp and t < 3:
                    CP(carry[:, c, 0:3], xbr[0][:, n:n + 3], [xbr[1]], ["carry%d" % c], eng="pool")
                if t == 3:
                    CP(stout[:, c, 0:3], xbr[0][:, n:n + 3], [xbr[1]], ["stout"], eng="pool")
                if samp:
                    CP(stout[:, c, 4:52].rearrange("p (s w) -> p s w", w=3), xb3[:, :, L:L + 3],
                       [xbr[1]], ["stout"], eng="pool")
                yield
                STT(gt[0][:, 0:n], gt[0][:, 0:n], 1.0, bky[0][:, 0:n], ALU.add, ALU.mult, [gt[1], bky[1]], [gt[1]])
                xc = tfp.get()
                xc3 = xc[0][:, 0:n].rearrange("p (s w) -> p s w", w=L)
                cw0 = PV_CW + j * 32 + c * 4
                TS(xc3, xb3[:, :, 0:L], pv[:, cw0:cw0 + 1], pv[:, PV_CB + pcol:PV_CB + pcol + 1],
                   ALU.mult, ALU.add, [xbr[1], "pv"], [xc[1]])
                ACT(gt[0][:, 0:n], gt[0][:, 0:n], AF.Tanh, [gt[1]], [gt[1]], scale=C_GELU)
                yield
                for tap in range(1, 4):
                    STT(xc3, xb3[:, :, tap:tap + L], pv[:, cw0 + tap:cw0 + tap + 1], xc3, ALU.mult, ALU.add,
                        [xbr[1], xc[1], "pv"], [xc[1]])
                tfp.put(xbr)
                xcb = tbp.get()
                CP(xcb[0][:, 0:n], xc[0][:, 0:n], [xc[1]], [xcb[1]], eng="pool")
                STT(gt[0][:, 0:n], gt[0][:, 0:n], 1.0, bky[0][:, 0:n], ALU.add, ALU.mult,
                    [gt[1], bky[1]], [gt[1]])
                banks.put(bky)
                yield
                bkr = banks.get()
                bki = banks.get()
                MM([(bkr[0][:, 0:n], gwa, xcb[0][:, 0:n], True, True)], [xcb[1], slot[1]], [bkr[1]])
                MM([(bki[0][:, 0:n], gwx, xcb[0][:, 0:n], True, True)], [xcb[1], slot[1]], [bki[1]])
                tbp.put(xcb)
                if t == len(TILES) - 1:
                    ws.release(slot)
                    del held[c]
                thr = tfp.get()
                ACT(thr[0][:, 0:n], bkr[0][:, 0:n], AF.Tanh, [bkr[1], "dv"], [thr[1]], scale=0.5,
                    bias=dv[:, pcol:pcol + 1])
                banks.put(bkr)
                thi = tfp.get()
                ACT(thi[0][:, 0:n], bki[0][:, 0:n], AF.Tanh, [bki[1], "dv"], [thi[1]], scale=0.5,
                    bias=dv[:, 16 + pcol:17 + pcol])
                banks.put(bki)
                yield
                ensure_proj(ui + 2)
                av = tfp.get()
                ACT(av[0][:, 0:n], thr[0][:, 0:n], AF.Exp, [thr[1], "dv"], [av[1]],
                    scale=dv[:, 32 + pcol:33 + pcol], bias=dv[:, 32 + pcol:33 + pcol])
                ACT(thr[0][:, 0:n], thr[0][:, 0:n], AF.Exp, [thr[1], "dv"], [thr[1]],
                    scale=dv[:, 48 + pcol:49 + pcol], bias=dv[:, 48 + pcol:49 + pcol])
                STT(thi[0][:, 0:n], thi[0][:, 0:n], 1.0, xc[0][:, 0:n], ALU.add, ALU.mult,
                    [thi[1], xc[1]], [thi[1]])
                tfp.put(xc)
                yield
                ACT(thr[0][:, 0:n], thr[0][:, 0:n], AF.Ln, [thr[1], "cst"], [thr[1]], scale=-1.0, bias=c_one)
                ACT(thr[0][:, 0:n], thr[0][:, 0:n], AF.Exp, [thr[1], "cst"], [thr[1]], scale=0.5, bias=c_lnh)
                yield
                TT(thi[0][:, 0:n], thi[0][:, 0:n], thr[0][:, 0:n], ALU.mult, [thi[1], thr[1]], [thi[1]])
                tfp.put(thr)
                if samp:
                    a3 = av[0][:, 0:n].rearrange("p (s w) -> p s w", w=L)
                    b3 = thi[0][:, 0:n].rearrange("p (s w) -> p s w", w=L)
                    h0v = st0[:, c, 48:64].rearrange("p (s w) -> p s w", w=1)
                    tmpv = small[:, 16:32].rearrange("p (s w) -> p s w", w=1)
                    TT(tmpv, a3[:, :, 0:1], h0v, ALU.mult, [av[1], "st0"], ["small"])
                    TT(b3[:, :, 0:1], b3[:, :, 0:1], tmpv, ALU.add, [thi[1], "small"], [thi[1]])
                    TT(a3[:, :, 0:1], a3[:, :, 0:1],
                       cmask[:, 512:640].rearrange("p (s w) -> p s w", w=L)[:, :, 0:1],
                       ALU.mult, [av[1], "cmask"], [av[1]])
                hb = tfp.get()
                if samp or t == 0:
                    init, icell = 0.0, []
                else:
                    init, icell = carry[:, c, 3:4], ["carry%d" % c]
                P.add("dve", (lambda o, d0, d1, ini: (lambda e: e.tensor_tensor_scan(
                    out=o, data0=d0, data1=d1, initial=ini, op0=ALU.mult, op1=ALU.add)))(
                    hb[0][:, 0:n], av[0][:, 0:n], thi[0][:, 0:n], init),
                    [av[1], thi[1]] + icell, [hb[1]])
                if not samp and t < 3:
                    CP(carry[:, c, 3:4], hb[0][:, n - 1:n], [hb[1]], ["carry%d" % c])
                tfp.put(av)
                tfp.put(thi)
                yield
                if t == 3:
                    CP(stout[:, c, 3:4], hb[0][:, n - 1:n], [hb[1]], ["stout"], eng="pool")
                if samp:
                    CP(stout[:, c, 52:68].rearrange("p (s w) -> p s w", w=1),
                       hb[0][:, 0:n].rearrange("p (s w) -> p s w", w=L)[:, :, L - 1:L],
                       [hb[1]], ["stout"], eng="pool")
                STT(actT[:, c, lo:lo + n], gt[0][:, 0:n], 0.5, hb[0][:, 0:n], ALU.mult, ALU.mult,
                    [gt[1], hb[1]], [acell(c, t)])
                tfp.put(gt)
                tfp.put(hb)

            run_lanes([unit(ui) for ui in range(len(ulist))], 2, 4)
            for hf in range(2):
                bk = banks.get()
                TR([(bk[0][0:68, q * 128:(q + 1) * 128], stout[:, hf * 4 + q, :], identf[:]) for q in range(4)],
                   ["stout", "identf"], [bk[1]])
                tfb = tfp.get()
                ACP(tfb[0][0:68, 0:512], bk[0][0:68, 0:512], [bk[1]], [tfb[1]])
                banks.put(bk)
                cs_ = slice(hf * 512, (hf + 1) * 512)
                DMA("sp", o_cp[j, :, cs_], tfb[0][0:3, 0:512], [tfb[1]], ["o_cp"])
                DMA("sp", o_hp[j:j + 1, cs_], tfb[0][3:4, 0:512], [tfb[1]], ["o_hp"])
                DMA("sp", o_cs[j, :, cs_], tfb[0][4:52, 0:512], [tfb[1]], ["o_cs"])
                DMA("sp", o_hs[j, :, cs_], tfb[0][52:68, 0:512], [tfb[1]], ["o_hs"])
                tfp.put(tfb)
            out_proj(["mixout%d_0" % l, "mixout%d_1" % l], actT, ALLA)

        def gla_block(l):
            j = l // 2
            extra = [(st0[:].rearrange("p c n -> p (c n)").bitcast(BF16), "st0"),
                     (stout[:].rearrange("p c n -> p (c n)").bitcast(BF16)[:, 0:1024], "stout")]
            for x_ in extra:
                tbp.put(x_)
            norm_to_xn(PV_MIX + l * 8)
            slot = ws.next("glalo%d" % l)
            wl = slot[0][:, 0:128].rearrange("p (k n) -> p k n", n=16)
            for t in range(len(TILES)):
                lo, n = tcols(t)
                bk = banks.get()
                MM([(bk[0][0:16, 0:n], wl[:, k, :], xnT[:, k, lo:lo + n], k == 0, k == NCH - 1)
                    for k in range(NCH)], [slot[1]] + ALLN(t), [bk[1]])
                ACP(actT[0:16, 7, lo:lo + n], bk[0][0:16, 0:n], [bk[1]], [acell(7, t)])
                banks.put(bk)
            ws.release(slot)
            QS = 128.0 ** -0.5
            chain_done = {}
            for h in range(4):
                sA = ws.next("glaA%d_%d" % (l, h))
                sB = ws.next("glaB%d_%d" % (l, h))
                wq_ = sA[0][:, 0:1024].rearrange("p (k n) -> p k n", n=128)
                wk_ = sA[0][:, 1024:2048].rearrange("p (k n) -> p k n", n=128)
                wv_ = sA[0][:, 2048:4096].rearrange("p (k n) -> p k n", n=256)
                wg_ = sB[0][:, 0:2048].rearrange("p (k n) -> p k n", n=256)
                MEMSET(Sf_p[:], 0.0, ["Sf_p"])
                MEMSET(Sb_p[:], 0.0, ["Sb_p"])
                def gunit(t, h=h, sA=sA, sB=sB, wq_=wq_, wk_=wk_, wv_=wv_, wg_=wg_):
                    lo, n = tcols(t)
                    samp = (t == 4)
                    L = 8 if samp else 64
                    nch = n // L
                    nsub = n // 128
                    cm = cmask[:, 512:640] if samp else cmask[:, 0:n]
                    am = amask[:, 128:256] if samp else amask[:, 0:128]
                    bq = banks.get()
                    bk_ = banks.get()
                    bz = banks.get()
                    MM([(bq[0][:, 0:n], wq_[:, k, :], xnT[:, k, lo:lo + n], k == 0, k == NCH - 1)
                        for k in range(NCH)], [sA[1]] + ALLN(t), [bq[1]])
                    MM([(bk_[0][:, 0:n], wk_[:, k, :], xnT[:, k, lo:lo + n], k == 0, k == NCH - 1)
                        for k in range(NCH)], [sA[1]] + ALLN(t), [bk_[1]])
                    MM([(bz[0][:, 0:n], wa2[0:16, j, h * 128:(h + 1) * 128], actT[0:16, 7, lo:lo + n], True, True)],
                       ["wa2", acell(7, t)], [bz[1]])
                    yield
                    spl = tfp.get()
                    gb = dv[:, 64 + j * 4 + h:65 + j * 4 + h]
                    ACT(spl[0][:, 0:n], bz[0][:, 0:n], AF.Exp, [bz[1], "dv"], [spl[1]], scale=-1.0, bias=gb)
                    banks.put(bz)
                    ACT(spl[0][:, 0:n], spl[0][:, 0:n], AF.Ln, [spl[1], "cst"], [spl[1]], bias=c_one)
                    cs = tfp.get()
                    P.add("dve", (lambda o, d0, d1: (lambda e: e.tensor_tensor_scan(
                        out=o, data0=d0, data1=d1, initial=0.0, op0=ALU.mult, op1=ALU.add)))(
                        cs[0][:, 0:n], cm, spl[0][:, 0:n]), [spl[1], "cmask"], [cs[1]])
                    yield
                    ACT(spl[0][:, 0:n], cs[0][:, 0:n], AF.Exp, [cs[1]], [spl[1]], scale=-1.0 / 16.0)
                    qin = tbp.get()
                    STT(qin[0][:, 0:n], bq[0][:, 0:n], QS, spl[0][:, 0:n], ALU.mult, ALU.mult,
                        [bq[1], spl[1]], [qin[1]])
                    banks.put(bq)
                    yield
                    ACT(spl[0][:, 0:n], cs[0][:, 0:n], AF.Exp, [cs[1]], [spl[1]], scale=1.0 / 16.0)
                    TT(qin[0][:, 512:512 + n], bk_[0][:, 0:n], spl[0][:, 0:n], ALU.mult,
                       [bk_[1], spl[1]], [qin[1]])
                    banks.put(bk_)
                    tfp.put(spl)
                    eg = tfp.get()
                    cs3 = cs[0][:, 0:n].rearrange("p (s w) -> p s w", w=L)
                    ACT(eg[0][:, 0:nch].rearrange("p (s w) -> p s w", w=1), cs3[:, :, L - 1:L], AF.Exp,
                        [cs[1]], [eg[1]], scale=-1.0 / 16.0)
                    tfp.put(cs)
                    kend = tbp.get()
                    TT(kend[0][:, 0:n].rearrange("p (s w) -> p s w", w=L),
                       qin[0][:, 512:512 + n].rearrange("p (s w) -> p s w", w=L),
                       eg[0][:, 0:nch].rearrange("p (s w) -> p s w", w=1).to_broadcast([128, nch, L]),
                       ALU.mult, [qin[1], eg[1]], [kend[1]])
                    yield
                    vt = tbp.get()
                    for s0 in range(0, nsub, 2):
                        bv = banks.get()
                        ns2 = min(2, nsub - s0)
                        mms = []
                        for s in range(s0, s0 + ns2):
                            mms += [(bv[0][:, (s - s0) * 256:(s - s0 + 1) * 256],
                                     xnT[:, k, lo + s * 128:lo + (s + 1) * 128], wv_[:, k, :],
                                     k == 0, k == NCH - 1) for k in range(NCH)]
                        MM(mms, [sA[1]] + ALLN(t), [bv[1]])
                        ACP(vt[0][:, s0 * 256:(s0 + ns2) * 256], bv[0][:, 0:ns2 * 256], [bv[1]], [vt[1]])
                        banks.put(bv)
                        yield
                    bt = banks.get()
                    btb = bt[0][:, 0:512].bitcast(BF16)
                    TR([(btb[:, s * 128:(s + 1) * 128], kend[0][:, s * 128:(s + 1) * 128], identb[:])
                        for s in range(nsub)], [kend[1], "identb"], [bt[1]])
                    CP(kend[0][:, 512:512 + n], btb[:, 0:n], [bt[1]], [kend[1]])
                    banks.put(bt)
                    yield
                    while 1 <= t <= 3 and not chain_done.get((h, t - 1), False):
                        yield
                    bo = [banks.get(), banks.get()]
                    for s in range(nsub):
                        ba = banks.get()
                        sc = slice(s * 128, (s + 1) * 128)
                        MM([(ba[0][:, 0:128], qin[0][:, 512 + s * 128:512 + (s + 1) * 128], qin[0][:, sc], True, True)],
                           [qin[1]], [ba[1]])
                        attm = tbp.get()
                        TT(attm[0][:, 0:128], ba[0][:, 0:128], am, ALU.mult, [ba[1], "amask"], [attm[1]])
                        banks.put(ba)
                        yield
                        MM([(bo[jv][0][:, sc], vt[0][:, s * 256 + jv * 128:s * 256 + (jv + 1) * 128],
                             attm[0][:, 0:128], True, False) for jv in range(2)],
                           [vt[1], attm[1]], [bo[0][1], bo[1][1]])
                        tbp.put(attm)
                        cps = nch // nsub
                        for ci in range(cps):
                            ch = s * cps + ci
                            col = slice(ch * L, (ch + 1) * L)
                            last = (ci == cps - 1)
                            if samp:
                                Sf = Sfs.get()
                                Sb = Sbs.get()
                                DMA("act", Sf[0][:], st_S[j, ch, h], [], [Sf[1]])
                                DMA("pool", Sb[0][:], st_S[j, ch, h], [], [Sb[1]])
                            else:
                                Sf = (Sf_p, "Sf_p")
                                Sb = (Sb_p, "Sb_p")
                            MM([(bo[jv][0][:, col], Sb[0][:, jv * 128:(jv + 1) * 128], qin[0][:, col], False, last)
                                for jv in range(2)], [Sb[1], qin[1]], [bo[0][1], bo[1][1]])
                            bs = banks.get()
                            if samp:
                                kem = tbp.get()
                                TS(kem[0][:, 0:128], kend[0][:, 512:640], chs[:, ch:ch + 1], 0.0, ALU.mult, ALU.add,
                                   [kend[1], "chs"], [kem[1]])
                                MM([(bs[0][:, 0:256], kem[0][:, 0:128], vt[0][:, 0:256], True, True)],
                                   [kem[1], vt[1]], [bs[1]])
                                tbp.put(kem)
                            else:
                                r0 = (ch % 2) * 64
                                MM([(bs[0][:, 0:256], kend[0][r0:r0 + 64, 512 + s * 128:512 + (s + 1) * 128],
                                     vt[0][r0:r0 + 64, s * 256:(s + 1) * 256], True, True)],
                                   [kend[1], vt[1]], [bs[1]])
                            if not samp:
                                STT(Sb[0][:], Sf[0][:], eg[0][:, ch:ch + 1], bs[0][:, 0:256], ALU.mult, ALU.add,
                                    [Sf[1], eg[1], bs[1]], [Sb[1]])
                            STT(Sf[0][:], Sf[0][:], eg[0][:, ch:ch + 1], bs[0][:, 0:256], ALU.mult, ALU.add,
                                [Sf[1], eg[1], bs[1]], [Sf[1]])
                            banks.put(bs)
                            yield
                            if samp:
                                DMA("sp", o_Ss[j, ch, h], Sf[0][:], [Sf[1]], ["o_Ss"])
                                Sfs.put(Sf)
                                Sbs.put(Sb)
                    chain_done[(h, t)] = True
                    tbp.put(qin)
                    tbp.put(kend)
                    tbp.put(vt)
                    tfp.put(eg)
                    if t == 3:
                        DMA("sp", o_Sp[j, h], Sf_p[:], ["Sf_p"], ["o_Sp"])
                    bm = banks.get()
                    for jv in range(2):
                        sq = tbp.get()
                        ACT(sq[0][:, 0:n], bo[jv][0][:, 0:n], AF.Square, [bo[jv][1]], [sq[1]])
                        MM([(bm[0][:, 0:n], ones256, sq[0][:, 0:n], jv == 0, jv == 1)], [sq[1], "ones_m"], [bm[1]])
                        tbp.put(sq)
                    rs = tfp.get()
                    ACT(rs[0][:, 0:n], bm[0][:, 0:n], AF.Ln, [bm[1], "cst"], [rs[1]], bias=c_eps)
                    banks.put(bm)
                    ACT(rs[0][:, 0:n], rs[0][:, 0:n], AF.Exp, [rs[1]], [rs[1]], scale=-0.5)
                    yield
                    for jv in range(2):
                        bg = banks.get()
                        MM([(bg[0][:, 0:n], wg_[:, k, jv * 128:(jv + 1) * 128], xnT[:, k, lo:lo + n],
                             k == 0, k == NCH - 1) for k in range(NCH)], [sB[1]] + ALLN(t), [bg[1]])
                        sg = tfp.get()
                        ACT(sg[0][:, 0:n], bg[0][:, 0:n], AF.Tanh, [bg[1]], [sg[1]], scale=0.5)
                        STT(sg[0][:, 0:n], sg[0][:, 0:n], 1.0, bg[0][:, 0:n], ALU.add, ALU.mult,
                            [sg[1], bg[1]], [sg[1]])
                        banks.put(bg)
                        yield
                        TT(sg[0][:, 0:n], sg[0][:, 0:n], rs[0][:, 0:n], ALU.mult, [sg[1], rs[1]], [sg[1]], eng="pool")
                        gcol = 72 + j * 2 + jv
                        STT(actT[:, 2 * h + jv, lo:lo + n], bo[jv][0][:, 0:n], dv[:, gcol:gcol + 1], sg[0][:, 0:n],
                            ALU.mult, ALU.mult, [bo[jv][1], sg[1], "dv"], [acell(2 * h + jv, t)])
                        tfp.put(sg)
                    tfp.put(rs)
                    banks.put(bo[0])
                    banks.put(bo[1])
                run_lanes([gunit(t) for t in range(len(TILES))], 2, 12)
                ws.release(sA)
                ws.release(sB)
            for x_ in extra:
                tbp.free.remove(x_)
            out_proj(["mixout%d_0" % l, "mixout%d_1" % l], actT, ALLA)

        def make_kT(slot):
            kT = [tbp.get(), tbp.get()]
            for half in range(2):
                for mt in range(2):
                    bt = banks.get()
                    btb = bt[0][:, 0:512].bitcast(BF16)
                    TR([(btb[:, q * 128:(q + 1) * 128],
                         slot[0][:, mt * 1024 + (half * 4 + q) * 128:mt * 1024 + (half * 4 + q + 1) * 128],
                         identb[:]) for q in range(4)], [slot[1], "identb"], [bt[1]])
                    if mt == 0:
                        CP(kT[half][0][:, 0:1024].rearrange("p (q m) -> p q m", m=256)[:, :, mt * 128:(mt + 1) * 128],
                           btb[:, 0:512].rearrange("p (q m) -> p q m", m=128), [bt[1]], [kT[half][1]])
                    else:
                        ACP(kT[half][0][:, 0:1024].rearrange("p (q m) -> p q m", m=256)[:, :, mt * 128:(mt + 1) * 128],
                            btb[:, 0:512].rearrange("p (q m) -> p q m", m=128), [bt[1]], [kT[half][1]])
                    banks.put(bt)
            return kT

        def attention(h, kT, slot, qfn, qcells, n, out_ap_fn, out_cells_fn):
            Vv = slot[0][:, 2048:4096].rearrange("p (b n) -> p b n", n=1024)
            bsc = [banks.get(), banks.get()]
            pT = tbp.get()
            for mt in range(2):
                mms = []
                for dc in range(2):
                    kc = 2 * h + dc
                    ksrc = kT[kc // 4][0][:, (kc % 4) * 256 + mt * 128:(kc % 4) * 256 + (mt + 1) * 128]
                    mms.append((bsc[mt][0][:, 0:n], ksrc, qfn(dc), dc == 0, dc == 1))
                MM(mms, [kT[0][1], kT[1][1]] + qcells, [bsc[mt][1]])
                ACT(pT[0][:, mt * 512:mt * 512 + n], bsc[mt][0][:, 0:n], AF.Exp, [bsc[mt][1]], [pT[1]],
                    scale=1.0 / 16.0)
                banks.put(bsc[mt])
            bd = banks.get()
            MM([(bd[0][:, 0:n], ones1, pT[0][:, mt * 512:mt * 512 + n], mt == 0, mt == 1) for mt in range(2)],
               [pT[1], "ones_m"], [bd[1]])
            rec = tfp.get()
            P.add("dve", (lambda o, i: (lambda e: e.reciprocal(out=o, in_=i)))(rec[0][:, 0:n], bd[0][:, 0:n]),
                  [bd[1]], [rec[1]])
            banks.put(bd)
            for jv in range(2):
                bp = banks.get()
                MM([(bp[0][:, 0:n], Vv[:, mt, (2 * h + jv) * 128:(2 * h + jv + 1) * 128],
                     pT[0][:, mt * 512:mt * 512 + n], mt == 0, mt == 1) for mt in range(2)],
                   [slot[1], pT[1]], [bp[1]])
                TT(out_ap_fn(jv), bp[0][:, 0:n], rec[0][:, 0:n], ALU.mult, [bp[1], rec[1]], out_cells_fn(jv))
                banks.put(bp)
            tbp.put(pT)
            tfp.put(rec)

        def attention_sample(sq_, kT, slot, qs, l):
            Vv = slot[0][:, 2048:4096].rearrange("p (b n) -> p b n", n=1024)
            bsc = banks.get()
            mms = []
            for mt in range(2):
                for h in range(4):
                    for dc in range(2):
                        kc = 2 * h + dc
                        ksrc = kT[kc // 4][0][:, (kc % 4) * 256 + mt * 128:(kc % 4) * 256 + (mt + 1) * 128]
                        mms.append((bsc[0][:, (mt * 4 + h) * 8:(mt * 4 + h + 1) * 8], ksrc,
                                    qs[0][:, kc * 128 + sq_ * 8:kc * 128 + (sq_ + 1) * 8], dc == 0, dc == 1))
            MM(mms, [kT[0][1], kT[1][1], qs[1]], [bsc[1]])
            pT = tbp.get()
            ACT(pT[0][:, 0:64], bsc[0][:, 0:64], AF.Exp, [bsc[1]], [pT[1]], scale=1.0 / 16.0)
            banks.put(bsc)
            bd = banks.get()
            MM([(bd[0][:, 0:32], ones1, pT[0][:, mt * 32:(mt + 1) * 32], mt == 0, mt == 1) for mt in range(2)],
               [pT[1], "ones_m"], [bd[1]])
            bp = banks.get()
            mms = []
            for h in range(4):
                for jv in range(2):
                    for mt in range(2):
                        mms.append((bp[0][:, (h * 2 + jv) * 8:(h * 2 + jv + 1) * 8],
                                    Vv[:, mt, (2 * h + jv) * 128:(2 * h + jv + 1) * 128],
                                    pT[0][:, (mt * 4 + h) * 8:(mt * 4 + h + 1) * 8], mt == 0, mt == 1))
            MM(mms, [slot[1], pT[1]], [bp[1]])
            tbp.put(pT)
            rec = tfp.get()
            P.add("dve", (lambda o, i: (lambda e: e.reciprocal(out=o, in_=i)))(rec[0][:, 0:32], bd[0][:, 0:32]),
                  [bd[1]], [rec[1]])
            banks.put(bd)
            col = slice(2048 + sq_ * 8, 2048 + (sq_ + 1) * 8)
            pv4 = bp[0][:, 0:64].rearrange("p (h j t) -> p h j t", j=2, t=8)
            for jv in range(2):
                TT(actT[:, :, col].rearrange("p (h j) t -> p h j t", j=2)[:, :, jv, :], pv4[:, :, jv, :],
                   rec[0][:, 0:32].rearrange("p (h t) -> p h t", t=8),
                   ALU.mult, [bp[1], rec[1]], [acell(2 * h + jv, 4) for h in range(4)])
            banks.put(bp)
            tfp.put(rec)

        def xa_block(l):
            extra = [(st0[:].rearrange("p c n -> p (c n)").bitcast(BF16), "st0"),
                     (stout[:].rearrange("p c n -> p (c n)").bitcast(BF16)[:, 0:1024], "stout")]
            for x_ in extra:
                tbp.put(x_)
            norm_to_xn(PV_XA + l * 8)
            kvslot = ws.next("kvp%d" % l)
            kT = make_kT(kvslot)
            qs = tbp.get()
            Vv = kvslot[0][:, 2048:4096].rearrange("p (b n) -> p b n", n=1024)
            held = {}
            units = [(h, t) for h in range(4) for t in range(len(TILES))]

            def s1(h, t):
                if t == 0:
                    held[h] = ws.next("wq%d_%d" % (l, h))
                slot = held[h]
                wq_ = slot[0][:, 0:2048].rearrange("p (k n) -> p k n", n=256)
                lo, n = tcols(t)
                qt = tbp.get() if t < 4 else None
                for dc in range(2):
                    bq = banks.get()
                    MM([(bq[0][:, 0:n], wq_[:, k, dc * 128:(dc + 1) * 128], xnT[:, k, lo:lo + n],
                         k == 0, k == NCH - 1) for k in range(NCH)], [slot[1]] + ALLN(t), [bq[1]])
                    if t < 4:
                        ACP(qt[0][:, dc * 512:dc * 512 + n], bq[0][:, 0:n], [bq[1]], [qt[1]])
                    else:
                        ACP(qs[0][:, (2 * h + dc) * 128:(2 * h + dc + 1) * 128], bq[0][:, 0:n], [bq[1]], [qs[1]])
                    banks.put(bq)
                if t == len(TILES) - 1:
                    ws.release(slot)
                    del held[h]
                return qt

            def s2(h, t, qt):
                lo, n = tcols(t)
                pT = tbp.get()
                for mt in range(2):
                    bsc = banks.get()
                    mms = []
                    for dc in range(2):
                        kc = 2 * h + dc
                        ksrc = kT[kc // 4][0][:, (kc % 4) * 256 + mt * 128:(kc % 4) * 256 + (mt + 1) * 128]
                        mms.append((bsc[0][:, 0:n], ksrc, qt[0][:, dc * 512:dc * 512 + n], dc == 0, dc == 1))
                    MM(mms, [kT[0][1], kT[1][1], qt[1]], [bsc[1]])
                    ACT(pT[0][:, mt * 512:mt * 512 + n], bsc[0][:, 0:n], AF.Exp, [bsc[1]], [pT[1]],
                        scale=1.0 / 16.0)
                    banks.put(bsc)
                tbp.put(qt)
                return pT

            def s3(h, t, pT):
                lo, n = tcols(t)
                bd = banks.get()
                MM([(bd[0][:, 0:n], ones1, pT[0][:, mt * 512:mt * 512 + n], mt == 0, mt == 1) for mt in range(2)],
                   [pT[1], "ones_m"], [bd[1]])
                rec = tfp.get()
                P.add("dve", (lambda o, i: (lambda e: e.reciprocal(out=o, in_=i)))(rec[0][:, 0:n], bd[0][:, 0:n]),
                      [bd[1]], [rec[1]])
                banks.put(bd)
                for jv in range(2):
                    bp = banks.get()
                    MM([(bp[0][:, 0:n], Vv[:, mt, (2 * h + jv) * 128:(2 * h + jv + 1) * 128],
                         pT[0][:, mt * 512:mt * 512 + n], mt == 0, mt == 1) for mt in range(2)],
                       [kvslot[1], pT[1]], [bp[1]])
                    TT(actT[:, 2 * h + jv, lo:lo + n], bp[0][:, 0:n], rec[0][:, 0:n], ALU.mult,
                       [bp[1], rec[1]], [acell(2 * h + jv, t)])
                    banks.put(bp)
                tbp.put(pT)
                tfp.put(rec)

            q1 = []
            q2 = []
            for (h, t) in units:
                qt = s1(h, t)
                if q2:
                    s3(*q2.pop(0))
                if q1:
                    hh, tt, qq = q1.pop(0)
                    q2.append((hh, tt, s2(hh, tt, qq)))
                if t < 4:
                    q1.append((h, t, qt))
            while q1 or q2:
                if q2:
                    s3(*q2.pop(0))
                if q1:
                    hh, tt, qq = q1.pop(0)
                    q2.append((hh, tt, s2(hh, tt, qq)))
            ws.release(kvslot)
            tbp.put(kT[0])
            tbp.put(kT[1])
            def sample_iter():
                pend = None
                for s in range(NSEQ_S + 1):
                    cur = None
                    if s < NSEQ_S:
                        kvslot = ws.next("kvs%d_%d" % (l, s))
                        cur = (s, make_kT(kvslot), kvslot)
                    if pend is not None:
                        ps_, pk_, pslot_ = pend
                        attention_sample(ps_, pk_, pslot_, qs, l)
                        ws.release(pslot_)
                        tbp.put(pk_[0])
                        tbp.put(pk_[1])
                    pend = cur
                    yield

            def oproj(slot, hf, tiles, hook):
                wv_ = slot[0][:, 0:4096].rearrange("p (k n) -> p k n", n=512)
                step = 0
                for oc4 in range(4):
                    oc = hf * 4 + oc4
                    for t in tiles:
                        lo, n = tcols(t)
                        bk = banks.get()
                        MM([(bk[0][:, 0:n], wv_[:, k, oc4 * 128:(oc4 + 1) * 128], actT[:, k, lo:lo + n],
                             k == 0, k == NCH - 1) for k in range(NCH)], [slot[1]] + ALLA(t), [bk[1]])
                        TT(xT[:, oc, lo:lo + n], xT[:, oc, lo:lo + n], bk[0][:, 0:n], ALU.add,
                           [bk[1], xcell(oc, t)], [xcell(oc, t)])
                        banks.put(bk)
                        step += 1
                        if hook is not None and step % 2 == 0:
                            hook()

            it = sample_iter()
            next(it)
            next(it)
            so0 = ws.next("wo%d_0" % l)
            oproj(so0, 0, range(4), lambda: next(it, None))
            ws.release(so0)
            so1 = ws.next("wo%d_1" % l)
            oproj(so1, 1, range(4), lambda: next(it, None))
            for _ in it:
                pass
            oproj(so1, 1, [4], None)
            ws.release(so1)
            so0 = ws.next("wo%d_0b" % l)
            oproj(so0, 0, [4], None)
            ws.release(so0)
            tbp.put(qs)
            for x_ in extra:
                tbp.free.remove(x_)

        def mlp_block(l):
            norm_to_xn(PV_MLP + l * 8)
            units = [(g, t) for g in range(8) for t in range(len(TILES))]
            held = {}

            def h_phase(g, t):
                if t == 0:
                    s1 = ws.next("w1_%d_%d" % (l, g))
                    s2 = ws.next("w2_%d_%d" % (l, g))
                    held[g] = (s1, s2)
                s1, s2 = held[g]
                w1v = s1[0][:, 0:4096].rearrange("p (k n) -> p k n", n=512)
                lo, n = tcols(t)
                hs = [tbp.get(), tbp.get()]
                for kk in range(4):
                    bh = banks.get()
                    MM([(bh[0][:, 0:n], w1v[:, k, kk * 128:(kk + 1) * 128], xnT[:, k, lo:lo + n],
                         k == 0, k == NCH - 1) for k in range(NCH)], [s1[1]] + ALLN(t), [bh[1]])
                    hv = hs[kk // 2][0][:, (kk % 2) * 512:(kk % 2) * 512 + n]
                    ACT(hv, bh[0][:, 0:n], AF.Relu, [bh[1]], [hs[kk // 2][1]])
                    banks.put(bh)
                    TT(hv, hv, hv, ALU.mult, [hs[kk // 2][1]], [hs[kk // 2][1]])
                return hs

            def y_phase(g, t, hs):
                s1, s2 = held[g]
                w2v = s2[0][:, 0:4096].rearrange("p (k n) -> p k n", n=1024)
                lo, n = tcols(t)
                for oc in range(NCH):
                    by = banks.get()
                    MM([(by[0][:, 0:n], w2v[:, kk, oc * 128:(oc + 1) * 128],
                         hs[kk // 2][0][:, (kk % 2) * 512:(kk % 2) * 512 + n], kk == 0, kk == 3)
                        for kk in range(4)], [s2[1], hs[0][1], hs[1][1]], [by[1]])
                    TT(xT[:, oc, lo:lo + n], xT[:, oc, lo:lo + n], by[0][:, 0:n], ALU.add,
                       [by[1], xcell(oc, t)], [xcell(oc, t)])
                    banks.put(by)
                tbp.put(hs[0])
                tbp.put(hs[1])
                if t == len(TILES) - 1:
                    ws.release(s1)
                    ws.release(s2)
                    del held[g]

            prev = None
            for (g, t) in units:
                if t == 0 and prev is not None:
                    y_phase(*prev)
                    prev = None
                hs = h_phase(g, t)
                if prev is not None:
                    y_phase(*prev)
                prev = (g, t, hs)
            y_phase(*prev)

        for l in range(DEPTH):
            if enabled("mix", l):
                if l % 2 == 0:
                    rg_block(l)
                else:
                    gla_block(l)
            if enabled("xa", l):
                use_slot_x(True)
                xa_block(l)
                use_slot_x(False)
            if enabled("mlp", l):
                mlp_block(l)

        for t in range(len(TILES)):
            lo, n = tcols(t)
            rs = norm_rstd(t)
            for s in range(n // 128):
                tok0 = lo + s * 128
                for hf in range(2):
                    yt = tfp.get()
                    for q in range(4):
                        c = hf * 4 + q
                        STT(yt[0][:, q * 128:(q + 1) * 128], xT[:, c, tok0:tok0 + 128],
                            pv[:, PV_FIN + c:PV_FIN + c + 1], rs[0][:, s * 128:(s + 1) * 128],
                            ALU.mult, ALU.mult, [xcell(c, t), rs[1], "pv"], [yt[1]])
                    bk = banks.get()
                    TR([(bk[0][:, q * 128:(q + 1) * 128], yt[0][:, q * 128:(q + 1) * 128], identf[:])
                        for q in range(4)], [yt[1], "identf"], [bk[1]])
                    tfp.put(yt)
                    yo = tfp.get()
                    ACP(yo[0][:, 0:512], bk[0][:, 0:512], [bk[1]], [yo[1]])
                    banks.put(bk)
                    if t < 4:
                        DMA("sp", o_yp[tok0:tok0 + 128, hf * 512:(hf + 1) * 512], yo[0][:, 0:512], [yo[1]], ["o_y"])
                    else:
                        DMA("sp", o_ys[:, hf * 512:(hf + 1) * 512], yo[0][:, 0:512], [yo[1]], ["o_y"])
                    tfp.put(yo)
            tfp.put(rs)

        assert ws.consumed == len(ws.plan), (ws.consumed, len(ws.plan))
        import os as _os
        if _os.environ.get("MK_VERBOSE"):
            print("sbuf bytes remaining", nc.sbuf_bytes_remaining, "ops", len(P.ops))
        P.emit(st)
    return nc


def _consts():
    c = np.zeros((128, 1168), np.float32)
    c[:, 0:128] = np.eye(128, dtype=np.float32)
    s = np.arange(128)[:, None]
    t = np.arange(128)[None, :]
    c[:, 128:256] = ((s <= t) & (s // 64 == t // 64)).astype(np.float32)
    c[:, 256:384] = ((s <= t) & (s // 8 == t // 8)).astype(np.float32)
    cm = np.ones((128, 640), np.float32)
    cm[:, 0:512:64] = 0.0
    cm[:, 512:640:8] = 0.0
    c[:, 384:1024] = cm
    c[:, 1024:1040] = (np.arange(128)[:, None] // 8 == np.arange(16)[None, :]).astype(np.float32)
    return c


def _pvec(inp):
    def fm(v):
        v = np.asarray(v, np.float32)
        lead = v.shape[:-1]
        n = v.shape[-1] // 128
        v = v.reshape(lead + (n, 128))
        v = np.moveaxis(v, -1, 0)
        return v.reshape(128, -1)
    p = np.zeros((128, 320), np.float32)
    p[:, 0:32] = fm(inp["norm_mix_g"])
    p[:, 32:64] = fm(inp["norm_xa_g"])
    p[:, 64:96] = fm(inp["norm_mem_g"])
    p[:, 96:128] = fm(inp["norm_mlp_g"])
    p[:, 128:136] = fm(inp["final_norm_g"])
    cw = np.asarray(inp["rg_conv_w"], np.float32)
    cw = cw.reshape(2, 4, 8, 128).transpose(3, 0, 2, 1)
    p[:, 136:200] = cw.reshape(128, 64)
    p[:, 200:216] = fm(inp["rg_conv_b"])
    p[:, 216:232] = fm(inp["rg_b_a"])
    p[:, 232:248] = fm(inp["rg_b_x"])
    p[:, 248:264] = fm(inp["rg_lambda"])
    p[:, 264:272] = fm(inp["gla_b_a"])
    p[:, 272:276] = fm(inp["gla_norm_g"])
    return p


_NC_CACHE = {}


def kernel(**inputs):
    inp = {k: np.asarray(v) for k, v in inputs.items()}
    import os
    stage = int(os.environ.get("MK_STAGE", "99"))
    ncores = int(os.environ.get("MK_CORES", str(NCORES)))
    if "nc" not in _NC_CACHE:
        _NC_CACHE["nc"] = build_program(stage)
    nc = _NC_CACHE["nc"]
    consts = _consts()
    pvec = _pvec(inp)
    shared = {
        "pvec": pvec, "consts": consts,
        "rg_w_in": inp["rg_w_in"], "rg_w_a": inp["rg_w_a"], "rg_w_x": inp["rg_w_x"],
        "rg_w_out": inp["rg_w_out"], "gla_w_in": inp["gla_w_in"], "gla_w_a2": inp["gla_w_a2"],
        "gla_w_out": inp["gla_w_out"], "xa_wq": inp["xa_wq"], "xa_wk": inp["xa_wk"],
        "xa_wv": inp["xa_wv"], "xa_wo": inp["xa_wo"], "mlp_w1": inp["mlp_w1"], "mlp_w2": inp["mlp_w2"],
    }
    in_maps = []
    for b in range(NCORES):
        sl = slice(b * 16, (b + 1) * 16)
        m = dict(shared)
        m["xp"] = np.ascontiguousarray(inp["x_prompt"][b])
        m["xs"] = np.ascontiguousarray(inp["x_sample"][sl].reshape(128, D))
        m["mem"] = np.ascontiguousarray(inp["mem_prompt"][b])
        m["st_h"] = np.ascontiguousarray(inp["state_rglru_h"][:, sl])
        m["st_conv"] = np.ascontiguousarray(inp["state_rglru_conv"][:, sl].reshape(2, 48, D))
        m["st_S"] = np.ascontiguousarray(inp["state_gla_S"][:, sl])
        m["ck"] = np.ascontiguousarray(inp["cache_mem_k"][:, sl].reshape(DEPTH, 16, 256, D))
        m["cv"] = np.ascontiguousarray(inp["cache_mem_v"][:, sl].reshape(DEPTH, 16, 256, D))
        in_maps.append(m)
    res = run_bass_kernel_spmd(nc, in_maps[:ncores], core_ids=list(range(ncores)))
    R = list(res.results)
    while len(R) < NCORES:
        R.append(R[0])
    f = np.float32
    y_prompt = np.stack([R[b]["o_yp"] for b in range(NCORES)]).astype(f)
    y_sample = np.concatenate([R[b]["o_ys"].reshape(16, 8, D) for b in range(NCORES)], 0).astype(f)
    mem_k = np.stack([R[b]["o_mk"].reshape(DEPTH, 256, 4, 256) for b in range(NCORES)], 1).astype(f)
    mem_v = np.stack([R[b]["o_mv"].reshape(DEPTH, 256, 4, 256) for b in range(NCORES)], 1).astype(f)
    h_p = np.stack([R[b]["o_hp"] for b in range(NCORES)], 1).astype(f)
    c_p = np.stack([R[b]["o_cp"] for b in range(NCORES)], 1).astype(f)
    S_p = np.stack([R[b]["o_Sp"] for b in range(NCORES)], 1).astype(f)
    h_s = np.concatenate([R[b]["o_hs"] for b in range(NCORES)], 1).astype(f)
    c_s = np.concatenate([R[b]["o_cs"].reshape(2, 16, 3, D) for b in range(NCORES)], 1).astype(f)
    S_s = np.concatenate([R[b]["o_Ss"] for b in range(NCORES)], 1).astype(f)
    return (y_prompt, y_sample, mem_k, mem_v, h_p, c_p, S_p, h_s, c_s, S_s)
```

```python
import math
from contextlib import ExitStack

import numpy as np
import concourse.bass as bass
import concourse.mybir as mybir
from concourse.bass_utils import run_bass_kernel_spmd

F32 = mybir.dt.float32
BF16 = mybir.dt.bfloat16
ALU = mybir.AluOpType
AF = mybir.ActivationFunctionType

NCORES = 8
D = 1024
NCH = 8
SEQ = 2048
NSEQ_S = 16
LS = 8
NT = SEQ + NSEQ_S * LS
TILES = [(0, 512), (512, 512), (1024, 512), (1536, 512), (2048, 128)]
DEPTH = 4
EPS = 1e-6
GLA_IN = 3088
NSLOT = 3
N_TF = 8
N_TB = 6
SEM_WRAP = 30000
ENGS = ("pe", "act", "dve", "pool", "sp")


class Op:
    __slots__ = ("eng", "fn", "reads", "writes", "dma", "idx", "deps", "signal",
                 "sem", "val", "prewait")

    def __init__(self, eng, fn, reads, writes, dma):
        self.eng = eng
        self.fn = fn
        self.reads = reads
        self.writes = writes
        self.dma = dma
        self.deps = []
        self.signal = False
        self.sem = None
        self.val = 0
        self.prewait = None


class Prog:
    def __init__(self, nc, n_dma_sems=16):
        self.nc = nc
        self.ops = []
        self.n_dma_sems = n_dma_sems

    def add(self, eng, fn, reads=(), writes=(), dma=False):
        op = Op(eng, fn, tuple(reads), tuple(writes), dma)
        op.idx = len(self.ops)
        self.ops.append(op)
        return op

    def _analyze(self):
        last_w = {}
        readers = {}
        ops = self.ops
        al = getattr(self, "alias", {})
        if al:
            for op in ops:
                if any(c in al for c in op.reads):
                    op.reads = tuple(x for c in op.reads for x in al.get(c, (c,)))
                if any(c in al for c in op.writes):
                    op.writes = tuple(x for c in op.writes for x in al.get(c, (c,)))
        for op in ops:
            deps = set()
            for c in op.reads:
                w = last_w.get(c)
                if w is not None:
                    deps.add(w)
            for c in op.writes:
                w = last_w.get(c)
                if w is not None:
                    deps.add(w)
                for r in readers.get(c, ()):
                    deps.add(r)
            deps.discard(op.idx)
            for c in op.reads:
                readers.setdefault(c, []).append(op.idx)
            for c in op.writes:
                last_w[c] = op.idx
                readers[c] = []
            dl = []
            for d in deps:
                dop = ops[d]
                if (not dop.dma) and (not op.dma) and dop.eng == "pe" and op.eng == "pe":
                    continue
                dl.append(d)
            op.deps = dl
            for d in dl:
                ops[d].signal = True

    def emit(self, stack):
        nc = self.nc
        self._analyze()
        cnt = {e: 0 for e in ENGS}
        eng_sems = {e: [] for e in ENGS}
        dma_sems = {e: [] for e in ENGS}
        dma_rr = {e: 0 for e in ENGS}
        dma_tot = {}
        for op in self.ops:
            if op.dma:
                lst = dma_sems[op.eng]
                k = dma_rr[op.eng] % self.n_dma_sems
                dma_rr[op.eng] += 1
                if k >= len(lst):
                    lst.append(stack.enter_context(nc.semaphore("d_%s_%d" % (op.eng, k))))
                s = lst[k]
                prev = dma_tot.get(s, 0)
                op.prewait = (s, prev) if prev > 0 else None
                dma_tot[s] = prev + 16
                op.sem = s
                op.val = prev + 16
            elif op.signal:
                c = cnt[op.eng]
                k = c // SEM_WRAP
                lst = eng_sems[op.eng]
                while k >= len(lst):
                    lst.append(stack.enter_context(nc.semaphore("e_%s_%d" % (op.eng, len(lst)))))
                op.sem = lst[k]
                op.val = c - k * SEM_WRAP + 1
                cnt[op.eng] = c + 1
        per_eng = {e: [o for o in self.ops if o.eng == e] for e in ENGS}
        ops = self.ops
        all_dma = list(dma_tot.items())

        def run(eng_name, eng):
            waited = {}
            for op in per_eng[eng_name]:
                need = {}
                if op.prewait is not None:
                    s, v = op.prewait
                    need[s] = max(need.get(s, 0), v)
                for d in op.deps:
                    dop = ops[d]
                    s, v = dop.sem, dop.val
                    if need.get(s, 0) < v:
                        need[s] = v
                for s, v in need.items():
                    if waited.get(s, 0) >= v:
                        continue
                    eng.wait_ge(s, v)
                    waited[s] = v
                ins = op.fn(eng)
                if op.dma:
                    ins.then_inc(op.sem, 16)
                elif op.signal:
                    ins.then_inc(op.sem, 1)
            if eng_name == "sp":
                for s, v in all_dma:
                    if waited.get(s, 0) < v:
                        eng.wait_ge(s, v)

        block = stack.enter_context(nc.Block())

        @block.tensor
        def _(e):
            run("pe", e)

        @block.scalar
        def _(e):
            run("act", e)

        @block.vector
        def _(e):
            run("dve", e)

        @block.gpsimd
        def _(e):
            run("pool", e)

        @block.sync
        def _(e):
            run("sp", e)


class FreeList:
    def __init__(self, items, name):
        self.free = list(items)
        self.name = name

    def get(self):
        assert self.free, "pool %s exhausted" % self.name
        return self.free.pop(0)

    def put(self, it):
        self.free.append(it)


def build_program(stage=None):
    nc = bass.Bass("TRN2", target_bir_lowering=False)
    if stage is None:
        stage = 99

    def enabled(kind, l=0):
        if kind == "memkv":
            return stage >= 1
        base = 2 + 3 * l
        return stage >= base + {"mix": 0, "xa": 1, "mlp": 2}[kind]

    def din(name, shape):
        return nc.dram_tensor(name, list(shape), F32, kind="ExternalInput").ap()

    def dout(name, shape):
        return nc.dram_tensor(name, list(shape), F32, kind="ExternalOutput").ap()

    xp = din("xp", [SEQ, D])
    xs = din("xs", [128, D])
    mem = din("mem", [256, D])
    st_h = din("st_h", [2, 16, D])
    st_conv = din("st_conv", [2, 48, D])
    st_S = din("st_S", [2, 16, 4, 128, 256])
    ck = din("ck", [DEPTH, 16, 256, D])
    cv = din("cv", [DEPTH, 16, 256, D])
    pvec = din("pvec", [128, 320])
    consts = din("consts", [128, 1168])
    rg_w_in = din("rg_w_in", [2, D, 2048])
    rg_w_a = din("rg_w_a", [2, 16, 64, 64])
    rg_w_x = din("rg_w_x", [2, 16, 64, 64])
    rg_w_out = din("rg_w_out", [2, D, D])
    gla_w_in = din("gla_w_in", [2, D, GLA_IN])
    gla_w_a2 = din("gla_w_a2", [2, 16, 512])
    gla_w_out = din("gla_w_out", [2, D, D])
    xa_wq = din("xa_wq", [DEPTH, D, D])
    xa_wk = din("xa_wk", [DEPTH, D, D])
    xa_wv = din("xa_wv", [DEPTH, D, D])
    xa_wo = din("xa_wo", [DEPTH, D, D])
    mlp_w1 = din("mlp_w1", [DEPTH, D, 4096])
    mlp_w2 = din("mlp_w2", [DEPTH, 4096, D])

    o_yp = dout("o_yp", [SEQ, D])
    o_ys = dout("o_ys", [128, D])
    o_mk = dout("o_mk", [DEPTH, 256, D])
    o_mv = dout("o_mv", [DEPTH, 256, D])
    o_hp = dout("o_hp", [2, D])
    o_cp = dout("o_cp", [2, 3, D])
    o_Sp = dout("o_Sp", [2, 4, 128, 256])
    o_hs = dout("o_hs", [2, 16, D])
    o_cs = dout("o_cs", [2, 48, D])
    o_Ss = dout("o_Ss", [2, 16, 4, 128, 256])

    st = ExitStack()
    with st:
        P = Prog(nc)

        def sb(name, shape, dt):
            return st.enter_context(nc.sbuf_tensor(name, list(shape), dt))

        xT = sb("xT", [128, NCH, NT], F32)
        xnT = sb("xnT", [128, NCH, NT], BF16)
        actT = sb("actT", [128, NCH, NT], BF16)
        slots = FreeList([(sb("wslot%d" % i, [128, 4096], BF16), "w%d" % i) for i in range(NSLOT)], "slots")
        tfbig = sb("tfbig", [128, N_TF * 516], F32)
        tf_all = [(tfbig[:, i * 516:(i + 1) * 516], "tf%d" % i) for i in range(N_TF)]
        tfp = FreeList(list(reversed(tf_all)), "tf")
        slot_x = (tfbig[:, 0:2048].bitcast(BF16), "wx")
        P.alias = {"wx": ["tf0", "tf1", "tf2", "tf3"]}

        def use_slot_x(on):
            if on:
                for it_ in tf_all[0:4]:
                    tfp.free.remove(it_)
                slots.free.append(slot_x)
            else:
                slots.free.remove(slot_x)
                for it_ in tf_all[0:4]:
                    tfp.free.append(it_)
        tbp = FreeList([(sb("tb%d" % i, [128, 1024], BF16), "tb%d" % i) for i in range(N_TB)], "tb")
        banks = FreeList([(st.enter_context(nc.psum_tensor("ps%d" % i, [128, 512], F32)), "ps%d" % i)
                          for i in range(8)], "psum")
        pv = sb("pv", [128, 276], F32)
        dv = sb("dv", [128, 96], F32)
        identf = sb("identf", [128, 128], F32)
        identb = sb("identb", [128, 128], BF16)
        amask = sb("amask", [128, 256], BF16)
        cmask = sb("cmask", [128, 640], F32)
        chs = sb("chs", [128, 16], F32)
        ones_m = sb("ones_m", [128, 384], BF16)
        cst = sb("cst", [128, 4], F32)
        wa2 = sb("wa2", [16, 2, 512], BF16)
        carry = sb("carry", [128, NCH, 4], F32)
        st0 = sb("st0", [128, NCH, 64], F32)
        stout = sb("stout", [128, NCH, 68], F32)
        Sf_p = sb("Sf_p", [128, 256], F32)
        Sb_p = sb("Sb_p", [128, 256], BF16)
        Sfs = FreeList([(sb("Sfs%d" % i, [128, 256], F32), "Sfs%d" % i) for i in range(4)], "Sfs")
        Sbs = FreeList([(sb("Sbs%d" % i, [128, 256], BF16), "Sbs%d" % i) for i in range(3)], "Sbs")
        small = sb("small", [128, 64], F32)

        def ACT(out, in_, func, reads, writes, scale=None, bias=None):
            kw = {}
            if scale is not None:
                kw["scale"] = scale
            if bias is not None:
                kw["bias"] = bias
            P.add("act", lambda e: e.activation(out=out, in_=in_, func=func, **kw), reads, writes)

        def TT(out, in0, in1, op, reads, writes, eng="dve"):
            P.add(eng, lambda e: e.tensor_tensor(out=out, in0=in0, in1=in1, op=op), reads, writes)

        def TS(out, in0, s1, s2, op0, op1, reads, writes, eng="dve"):
            P.add(eng, lambda e: e.tensor_scalar(out=out, in0=in0, scalar1=s1, scalar2=s2, op0=op0, op1=op1),
                  reads, writes)

        def STT(out, in0, scalar, in1, op0, op1, reads, writes, eng="dve"):
            P.add(eng, lambda e: e.scalar_tensor_tensor(out=out, in0=in0, scalar=scalar, in1=in1,
                                                         op0=op0, op1=op1), reads, writes)

        def CP(out, in_, reads, writes, eng="dve"):
            P.add(eng, lambda e: e.tensor_copy(out=out, in_=in_), reads, writes)

        def ACP(out, in_, reads, writes):
            ACT(out, in_, AF.Copy, reads, writes)

        def MM(mms, reads, writes):
            def fn(e):
                ins = None
                for (o, l, r, s1, s2) in mms:
                    ins = e.matmul(o, lhsT=l, rhs=r, start=s1, stop=s2)
                return ins
            P.add("pe", fn, reads, writes)

        def TR(items, reads, writes):
            def fn(e):
                ins = None
                for (o, i, idn) in items:
                    ins = e.transpose(out=o, in_=i, identity=idn)
                return ins
            P.add("pe", fn, reads, writes)

        def DMA(q, out, in_, reads, writes):
            P.add(q, lambda e: e.dma_start(out=out, in_=in_), reads, writes, dma=True)

        def MEMSET(ap, val, writes, eng="pool"):
            P.add(eng, lambda e: e.memset(ap, val), (), writes)

        def run_lanes(gens, nlanes, lag):
            queue = list(gens)
            lanes = [None] * nlanes
            delay = [i * lag for i in range(nlanes)]
            while queue or any(g is not None for g in lanes):
                for i in range(nlanes):
                    if delay[i] > 0:
                        delay[i] -= 1
                        continue
                    if lanes[i] is None:
                        if not queue:
                            continue
                        lanes[i] = queue.pop(0)
                    try:
                        next(lanes[i])
                    except StopIteration:
                        lanes[i] = None

        def xcell(c, t):
            return "x%d_%d" % (c, t)

        def ncell(c, t):
            return "n%d_%d" % (c, t)

        def acell(c, t):
            return "a%d_%d" % (c, t)

        ALLN = lambda t: [ncell(c, t) for c in range(NCH)]
        ALLA = lambda t: [acell(c, t) for c in range(NCH)]

        class WS:
            def __init__(self):
                self.plan = []
                self.issued = 0
                self.consumed = 0
                self.loaded = {}
                self.ahead = 2

            def _issue(self, want=None):
                while (self.issued < len(self.plan) and slots.free
                       and self.issued - self.consumed < self.ahead):
                    tag, fn = self.plan[self.issued]
                    if tag.startswith("kvp") and tag != want:
                        break
                    okx = tag.startswith(("kvs", "wo"))
                    cand = [x for x in slots.free if okx or x[1] != "wx"]
                    if not cand:
                        break
                    slot = cand[0]
                    slots.free.remove(slot)
                    for (o, i, rd) in fn(slot[0]):
                        if isinstance(i, float):
                            MEMSET(o, i, [slot[1]])
                        else:
                            DMA("pool", o, i, rd, [slot[1]])
                    self.loaded[self.issued] = slot
                    self.issued += 1

            def next(self, tag):
                self._issue(tag)
                assert self.consumed < self.issued, "weight stream starved at %s" % tag
                assert self.plan[self.consumed][0] == tag, (self.plan[self.consumed][0], tag)
                slot = self.loaded.pop(self.consumed)
                self.consumed += 1
                self._issue()
                return slot

            def release(self, slot):
                slots.put(slot)

        ws = WS()

        def piece_cols(W, col0, ncols, dst0=0):
            def fn(slot):
                o = slot[:, dst0:dst0 + 8 * ncols].rearrange("p (k n) -> p k n", n=ncols)
                i = W[:, col0:col0 + ncols].rearrange("(k p) n -> p k n", p=128)
                return [(o, i, [])]
            return fn

        def multi(*fns):
            def fn(slot):
                r = []
                for f in fns:
                    r += f(slot)
                return r
            return fn

        def build_plan():
            plan = []
            for l in range(DEPTH):
                if enabled("memkv"):
                    for hf in range(2):
                        plan.append(("wk%d_%d" % (l, hf), piece_cols(xa_wk[l], hf * 512, 512)))
                    for hf in range(2):
                        plan.append(("wv%d_%d" % (l, hf), piece_cols(xa_wv[l], hf * 512, 512)))
            for l in range(DEPTH):
                j = l // 2
                if not enabled("mix", l):
                    pass
                elif l % 2 == 0:
                    for c in range(NCH):
                        def gates(c=c, j=j):
                            def fn(slot):
                                r = [(slot[:, 2048:2304], 0.0, [])]
                                for g, W in ((0, rg_w_a), (1, rg_w_x)):
                                    for hb in range(2):
                                        r.append((slot[hb * 64:(hb + 1) * 64,
                                                       2048 + g * 128 + hb * 64:2048 + g * 128 + (hb + 1) * 64],
                                                  W[j, 2 * c + hb], []))
                                return r
                            return fn
                        plan.append(("rgin%d_%d" % (l, c), multi(
                            piece_cols(rg_w_in[j], c * 128, 128, 0),
                            piece_cols(rg_w_in[j], 1024 + c * 128, 128, 1024),
                            gates())))
                    for hf in range(2):
                        plan.append(("mixout%d_%d" % (l, hf), piece_cols(rg_w_out[j], hf * 512, 512)))
                else:
                    plan.append(("glalo%d" % l, piece_cols(gla_w_in[j], 3072, 16)))
                    for h in range(4):
                        plan.append(("glaA%d_%d" % (l, h), multi(
                            piece_cols(gla_w_in[j], h * 128, 128, 0),
                            piece_cols(gla_w_in[j], 512 + h * 128, 128, 1024),
                            piece_cols(gla_w_in[j], 1024 + h * 256, 256, 2048))))
                        plan.append(("glaB%d_%d" % (l, h), piece_cols(gla_w_in[j], 2048 + h * 256, 256)))
                    for hf in range(2):
                        plan.append(("mixout%d_%d" % (l, hf), piece_cols(gla_w_out[j], hf * 512, 512)))

                def kvp(l=l):
                    def fn(slot):
                        ok = slot[:, 0:2048].rearrange("p (b n) -> p b n", n=1024)
                        ov = slot[:, 2048:4096].rearrange("p (b n) -> p b n", n=1024)
                        return [(ok, o_mk[l].rearrange("(b p) n -> p b n", p=128), ["memk%d" % l]),
                                (ov, o_mv[l].rearrange("(b p) n -> p b n", p=128), ["memv%d" % l])]
                    return fn
                if enabled("xa", l):
                    plan.append(("kvp%d" % l, kvp()))
                    for h in range(4):
                        plan.append(("wq%d_%d" % (l, h), piece_cols(xa_wq[l], h * 256, 256)))

                def kvs(s, l=l):
                    def fn(slot):
                        ok = slot[:, 0:2048].rearrange("p (b n) -> p b n", n=1024)
                        ov = slot[:, 2048:4096].rearrange("p (b n) -> p b n", n=1024)
                        return [(ok, ck[l, s].rearrange("(b p) n -> p b n", p=128), []),
                                (ov, cv[l, s].rearrange("(b p) n -> p b n", p=128), [])]
                    return fn
                for s in range(NSEQ_S):
                    if enabled("xa", l):
                        plan.append(("kvs%d_%d" % (l, s), kvs(s)))
                        if s == 1:
                            plan.append(("wo%d_0" % l, piece_cols(xa_wo[l], 0, 512)))
                        if s == 9:
                            plan.append(("wo%d_1" % l, piece_cols(xa_wo[l], 512, 512)))
                if enabled("xa", l):
                    plan.append(("wo%d_0b" % l, piece_cols(xa_wo[l], 0, 512)))
                for g in range(8):
                    if not enabled("mlp", l):
                        break
                    plan.append(("w1_%d_%d" % (l, g), piece_cols(mlp_w1[l], g * 512, 512)))

                    def w2p(g=g, l=l):
                        def fn(slot):
                            o = slot[:, 0:4096].rearrange("p (k n) -> p k n", n=1024)
                            i = mlp_w2[l, g * 512:(g + 1) * 512, :].rearrange("(k p) n -> p k n", p=128)
                            return [(o, i, [])]
                        return fn
                    plan.append(("w2_%d_%d" % (l, g), w2p()))
            return plan

        ws.plan = build_plan()

        DMA("sp", pv[:], pvec[:, 0:276], [], ["pv"])
        DMA("sp", identf[:], consts[:, 0:128], [], ["identf"])
        DMA("sp", cmask[:], consts[:, 384:1024], [], ["cmask"])
        DMA("sp", chs[:], consts[:, 1024:1040], [], ["chs"])
        DMA("pool", identb[:], consts[:, 0:128], [], ["identb"])
        DMA("pool", amask[:], consts[:, 128:384], [], ["amask"])
        DMA("pool", wa2[:], gla_w_a2.rearrange("j r n -> r j n"), [], ["wa2"])
        MEMSET(ones_m[:, 0:128], 1.0 / 1024.0, ["ones_m"])
        MEMSET(ones_m[:, 128:256], 1.0 / 256.0, ["ones_m"])
        MEMSET(ones_m[:, 256:384], 1.0, ["ones_m"])
        MEMSET(cst[:, 0:1], EPS, ["cst"])
        MEMSET(cst[:, 1:2], 1.0, ["cst"])
        MEMSET(cst[:, 2:3], math.log(0.5), ["cst"])
        MEMSET(cst[:, 3:4], 0.0, ["cst"])
        c_eps = cst[:, 0:1]
        c_one = cst[:, 1:2]
        c_lnh = cst[:, 2:3]
        ones1024 = ones_m[:, 0:128]
        ones256 = ones_m[:, 128:256]
        ones1 = ones_m[:, 256:384]

        PV_MIX, PV_XA, PV_MEM, PV_MLP, PV_FIN = 0, 32, 64, 96, 128
        PV_CW, PV_CB, PV_BA, PV_BX, PV_LAM = 136, 200, 216, 232, 248
        PV_GBA, PV_GNG = 264, 272
        TS(dv[:, 0:16], pv[:, PV_BA:PV_BA + 16], 0.5, 0.0, ALU.mult, ALU.add, ["pv"], ["dv"])
        TS(dv[:, 16:32], pv[:, PV_BX:PV_BX + 16], 0.5, 0.0, ALU.mult, ALU.add, ["pv"], ["dv"])
        ACT(dv[:, 32:48], pv[:, PV_LAM:PV_LAM + 16], AF.Exp, ["pv"], ["dv"], scale=-1.0)
        ACT(dv[:, 32:48], dv[:, 32:48], AF.Ln, ["dv", "cst"], ["dv"], bias=c_one)
        TS(dv[:, 48:64], dv[:, 32:48], -8.0, 0.0, ALU.mult, ALU.add, ["dv"], ["dv"])
        TS(dv[:, 32:48], dv[:, 32:48], -4.0, 0.0, ALU.mult, ALU.add, ["dv"], ["dv"])
        TS(dv[:, 64:72], pv[:, PV_GBA:PV_GBA + 8], -1.0, 0.0, ALU.mult, ALU.add, ["pv"], ["dv"])

        def tcols(t):
            lo, n = TILES[t]
            return lo, n

        def load_rows_T(src_rows, tok0, t):
            for hf in range(2):
                tfb = tfp.get()
                DMA("sp", tfb[0][:, 0:512], src_rows[:, hf * 512:(hf + 1) * 512], [], [tfb[1]])
                bk = banks.get()
                TR([(bk[0][:, q * 128:(q + 1) * 128], tfb[0][:, q * 128:(q + 1) * 128], identf[:])
                    for q in range(4)], [tfb[1], "identf"], [bk[1]])
                tfp.put(tfb)
                ACP(xT[:, hf * 4:hf * 4 + 4, tok0:tok0 + 128],
                    bk[0][:, 0:512].rearrange("p (q n) -> p q n", n=128),
                    [bk[1]], [xcell(c, t) for c in range(hf * 4, hf * 4 + 4)])
                banks.put(bk)

        for i in range(16):
            load_rows_T(xp[i * 128:(i + 1) * 128, :], i * 128, i // 4)
        load_rows_T(xs[:, :], 2048, 4)

        def norm_rstd(t):
            lo, n = tcols(t)
            bk = banks.get()
            for c in range(NCH):
                sq = tbp.get()
                ACT(sq[0][:, 0:n], xT[:, c, lo:lo + n], AF.Square, [xcell(c, t)], [sq[1]])
                MM([(bk[0][:, 0:n], ones1024, sq[0][:, 0:n], c == 0, c == NCH - 1)],
                   [sq[1], "ones_m"], [bk[1]])
                tbp.put(sq)
            rs = tfp.get()
            ACT(rs[0][:, 0:n], bk[0][:, 0:n], AF.Ln, [bk[1], "cst"], [rs[1]], bias=c_eps)
            banks.put(bk)
            ACT(rs[0][:, 0:n], rs[0][:, 0:n], AF.Exp, [rs[1]], [rs[1]], scale=-0.5)
            return rs

        def norm_to_xn(gcol):
            for t in range(len(TILES)):
                lo, n = tcols(t)
                rs = norm_rstd(t)
                for c in range(NCH):
                    STT(xnT[:, c, lo:lo + n], xT[:, c, lo:lo + n], pv[:, gcol + c:gcol + c + 1],
                        rs[0][:, 0:n], ALU.mult, ALU.mult, [xcell(c, t), rs[1], "pv"], [ncell(c, t)])
                tfp.put(rs)

        def out_proj(tags, src, src_cells):
            for hf in range(2):
                slot = ws.next(tags[hf])
                wv_ = slot[0][:, 0:4096].rearrange("p (k n) -> p k n", n=512)
                for oc4 in range(4):
                    oc = hf * 4 + oc4
                    for t in range(len(TILES)):
                        lo, n = tcols(t)
                        bk = banks.get()
                        MM([(bk[0][:, 0:n], wv_[:, k, oc4 * 128:(oc4 + 1) * 128], src[:, k, lo:lo + n],
                             k == 0, k == NCH - 1) for k in range(NCH)],
                           [slot[1]] + src_cells(t), [bk[1]])
                        TT(xT[:, oc, lo:lo + n], xT[:, oc, lo:lo + n], bk[0][:, 0:n], ALU.add,
                           [bk[1], xcell(oc, t)], [xcell(oc, t)])
                        banks.put(bk)
                ws.release(slot)

        def mem_kv():
            rstd_m = small[:, 0:2]
            for b in range(2):
                for hf in range(2):
                    tfb = tfp.get()
                    DMA("sp", tfb[0][:, 0:512], mem[b * 128:(b + 1) * 128, hf * 512:(hf + 1) * 512], [], [tfb[1]])
                    junk = tfp.get()
                    ACT(junk[0][:, 0:512], tfb[0][:, 0:512], AF.Square, [tfb[1]], [junk[1], "small"],)
                    P.add("dve", (lambda o, i: (lambda e: e.reduce_sum(out=o, in_=i, axis=mybir.AxisListType.X)))(
                        small[:, 4 + b * 2 + hf:5 + b * 2 + hf], junk[0][:, 0:512]), [junk[1]], ["small"])
                    tfp.put(junk)
                    tfp.put(tfb)
                TT(small[:, 8 + b:9 + b], small[:, 4 + b * 2:5 + b * 2], small[:, 5 + b * 2:6 + b * 2], ALU.add,
                   ["small"], ["small"])
            TS(small[:, 0:2], small[:, 8:10], 1.0 / 1024.0, 0.0, ALU.mult, ALU.add, ["small"], ["small"])
            ACT(small[:, 0:2], small[:, 0:2], AF.Ln, ["small", "cst"], ["small"], bias=c_eps)
            ACT(small[:, 0:2], small[:, 0:2], AF.Exp, ["small"], ["small"], scale=-0.5)
        def mem_kv_layer(l):
            rstd_m = small[:, 0:2]
            if True:
                mn = [tbp.get(), tbp.get()]
                for b in range(2):
                    for hf in range(2):
                        tfb = tfp.get()
                        DMA("sp", tfb[0][:, 0:512], mem[b * 128:(b + 1) * 128, hf * 512:(hf + 1) * 512], [], [tfb[1]])
                        TS(tfb[0][:, 0:512], tfb[0][:, 0:512], rstd_m[:, b:b + 1], 0.0, ALU.mult, ALU.add,
                           [tfb[1], "small"], [tfb[1]])
                        bk = banks.get()
                        TR([(bk[0][:, q * 128:(q + 1) * 128], tfb[0][:, q * 128:(q + 1) * 128], identf[:])
                            for q in range(4)], [tfb[1], "identf"], [bk[1]])
                        tfp.put(tfb)
                        for q in range(4):
                            c = hf * 4 + q
                            mview = mn[hf][0][:, q * 256 + b * 128:q * 256 + (b + 1) * 128]
                            ACT(mview, bk[0][:, q * 128:(q + 1) * 128], AF.Copy, [bk[1], "pv"], [mn[hf][1]],
                                scale=pv[:, PV_MEM + l * 8 + c:PV_MEM + l * 8 + c + 1])
                        banks.put(bk)

                def mnT(k, b):
                    return mn[k // 4][0][:, (k % 4) * 256 + b * 128:(k % 4) * 256 + (b + 1) * 128]
                for (nm, dst, cellp) in (("wk", o_mk, "memk"), ("wv", o_mv, "memv")):
                    for hf in range(2):
                        slot = ws.next("%s%d_%d" % (nm, l, hf))
                        wv_ = slot[0][:, 0:4096].rearrange("p (k n) -> p k n", n=512)
                        for b in range(2):
                            bk = banks.get()
                            MM([(bk[0][:, 0:512], mnT(k, b), wv_[:, k, :], k == 0, k == NCH - 1)
                                for k in range(NCH)], [slot[1], mn[0][1], mn[1][1]], [bk[1]])
                            tfb = tfp.get()
                            ACP(tfb[0][:, 0:512], bk[0][:, 0:512], [bk[1]], [tfb[1]])
                            banks.put(bk)
                            DMA("sp", dst[l, b * 128:(b + 1) * 128, hf * 512:(hf + 1) * 512], tfb[0][:, 0:512],
                                [tfb[1]], ["%s%d" % (cellp, l)])
                            tfp.put(tfb)
                        ws.release(slot)
                tbp.put(mn[0])
                tbp.put(mn[1])

        if enabled("memkv"):
            mem_kv()
            for l in range(DEPTH):
                mem_kv_layer(l)

        C_GELU = math.sqrt(2.0 / math.pi)

        def rg_block(l):
            j = l // 2
            for hf in range(2):
                tfb = tfp.get()
                DMA("sp", tfb[0][0:48, 0:512], st_conv[j, :, hf * 512:(hf + 1) * 512], [], [tfb[1]])
                DMA("sp", tfb[0][48:64, 0:512], st_h[j, :, hf * 512:(hf + 1) * 512], [], [tfb[1]])
                bk = banks.get()
                TR([(bk[0][:, q * 64:(q + 1) * 64], tfb[0][0:64, q * 128:(q + 1) * 128], identf[0:64, 0:64])
                    for q in range(4)], [tfb[1], "identf"], [bk[1]])
                tfp.put(tfb)
                ACP(st0[:, hf * 4:hf * 4 + 4, :], bk[0][:, 0:256].rearrange("p (q n) -> p q n", n=64),
                    [bk[1]], ["st0"])
                banks.put(bk)
            norm_to_xn(PV_MIX + l * 8)
            held = {}

            ulist = [(c, t) for c in range(NCH) for t in range(len(TILES))]
            projd = {}
            hlast = {}

            def ensure_proj(ui):
                if ui >= len(ulist) or ui in projd:
                    return
                c, t = ulist[ui]
                if t == 0:
                    held[c] = ws.next("rgin%d_%d" % (l, c))
                slot = held[c]
                wy = slot[0][:, 0:1024].rearrange("p (k n) -> p k n", n=128)
                wx = slot[0][:, 1024:2048].rearrange("p (k n) -> p k n", n=128)
                lo, n = tcols(t)
                bky = banks.get()
                bkx = banks.get()
                MM([(bky[0][:, 0:n], wy[:, k, :], xnT[:, k, lo:lo + n], k == 0, k == NCH - 1)
                    for k in range(NCH)], [slot[1]] + ALLN(t), [bky[1]])
                MM([(bkx[0][:, 0:n], wx[:, k, :], xnT[:, k, lo:lo + n], k == 0, k == NCH - 1)
                    for k in range(NCH)], [slot[1]] + ALLN(t), [bkx[1]])
                projd[ui] = (bky, bkx)

            def unit(ui):
                c, t = ulist[ui]
                ensure_proj(ui)
                bky, bkx = projd.pop(ui)
                slot = held[c]
                gwa = slot[0][:, 2048:2176]
                gwx = slot[0][:, 2176:2304]
                pcol = j * 8 + c
                lo, n = tcols(t)
                samp = (t == 4)
                nseg, L = (16, 8) if samp else (1, n)
                W_ = nseg * (L + 3)
                gt = tfp.get()
                ACT(gt[0][:, 0:n], bky[0][:, 0:n], AF.Square, [bky[1]], [gt[1]], scale=math.sqrt(0.044715))
                xbr = tfp.get()
                xb3 = xbr[0][:, 0:W_].rearrange("p (s w) -> p s w", w=L + 3)
                CP(xb3[:, :, 3:3 + L], bkx[0][:, 0:n].rearrange("p (s w) -> p s w", w=L),
                   [bkx[1]], [xbr[1]])
                banks.put(bkx)
                if samp:
                    CP(xb3[:, :, 0:3], st0[:, c, 0:48].rearrange("p (s w) -> p s w", w=3),
                       ["st0"], [xbr[1]], eng="pool")
                elif t == 0:
                    MEMSET(xbr[0][:, 0:3], 0.0, [xbr[1]])
                else:
                    CP(xbr[0][:, 0:3], carry[:, c, 0:3], ["carry%d" % c], [xbr[1]], eng="pool")
                if not samp and t < 3:
                    CP(carry[:, c, 0:3], xbr[0][:, n:n + 3], [xbr[1]], ["carry%d" % c], eng="pool")
                if t == 3:
                    CP(stout[:, c, 0:3], xbr[0][:, n:n + 3], [xbr[1]], ["stout"], eng="pool")
                if samp:
                    CP(stout[:, c, 4:52].rearrange("p (s w) -> p s w", w=3), xb3[:, :, L:L + 3],
                       [xbr[1]], ["stout"], eng="pool")
                yield
                STT(gt[0][:, 0:n], gt[0][:, 0:n], 1.0, bky[0][:, 0:n], ALU.add, ALU.mult, [gt[1], bky[1]], [gt[1]])
                xc = tfp.get()
                xc3 = xc[0][:, 0:n].rearrange("p (s w) -> p s w", w=L)
                cw0 = PV_CW + j * 32 + c * 4
                TS(xc3, xb3[:, :, 0:L], pv[:, cw0:cw0 + 1], pv[:, PV_CB + pcol:PV_CB + pcol + 1],
                   ALU.mult, ALU.add, [xbr[1], "pv"], [xc[1]])
                ACT(gt[0][:, 0:n], gt[0][:, 0:n], AF.Tanh, [gt[1]], [gt[1]], scale=C_GELU)
                yield
                for tap in range(1, 4):
                    STT(xc3, xb3[:, :, tap:tap + L], pv[:, cw0 + tap:cw0 + tap + 1], xc3, ALU.mult, ALU.add,
                        [xbr[1], xc[1], "pv"], [xc[1]])
                tfp.put(xbr)
                xcb = tbp.get()
                CP(xcb[0][:, 0:n], xc[0][:, 0:n], [xc[1]], [xcb[1]], eng="pool")
                STT(gt[0][:, 0:n], gt[0][:, 0:n], 1.0, bky[0][:, 0:n], ALU.add, ALU.mult,
                    [gt[1], bky[1]], [gt[1]])
                banks.put(bky)
                yield
                bkr = banks.get()
                bki = banks.get()
                MM([(bkr[0][:, 0:n], gwa, xcb[0][:, 0:n], True, True)], [xcb[1], slot[1]], [bkr[1]])
                MM([(bki[0][:, 0:n], gwx, xcb[0][:, 0:n], True, True)], [xcb[1], slot[1]], [bki[1]])
                tbp.put(xcb)
                if t == len(TILES) - 1:
                    ws.release(slot)
                    del held[c]
                thr = tfp.get()
                ACT(thr[0][:, 0:n], bkr[0][:, 0:n], AF.Tanh, [bkr[1], "dv"], [thr[1]], scale=0.5,
                    bias=dv[:, pcol:pcol + 1])
                banks.put(bkr)
                thi = tfp.get()
                ACT(thi[0][:, 0:n], bki[0][:, 0:n], AF.Tanh, [bki[1], "dv"], [thi[1]], scale=0.5,
                    bias=dv[:, 16 + pcol:17 + pcol])
                banks.put(bki)
                yield
                ensure_proj(ui + 2)
                av = tfp.get()
                ACT(av[0][:, 0:n], thr[0][:, 0:n], AF.Exp, [thr[1], "dv"], [av[1]],
                    scale=dv[:, 32 + pcol:33 + pcol], bias=dv[:, 32 + pcol:33 + pcol])
                ACT(thr[0][:, 0:n], thr[0][:, 0:n], AF.Exp, [thr[1], "dv"], [thr[1]],
                    scale=dv[:, 48 + pcol:49 + pcol], bias=dv[:, 48 + pcol:49 + pcol])
                STT(thi[0][:, 0:n], thi[0][:, 0:n], 1.0, xc[0][:, 0:n], ALU.add, ALU.mult,
                    [thi[1], xc[1]], [thi[1]])
                tfp.put(xc)
                yield
                ACT(thr[0][:, 0:n], thr[0][:, 0:n], AF.Ln, [thr[1], "cst"], [thr[1]], scale=-1.0, bias=c_one)
                ACT(thr[0][:, 0:n], thr[0][:, 0:n], AF.Exp, [thr[1], "cst"], [thr[1]], scale=0.5, bias=c_lnh)
                yield
                TT(thi[0][:, 0:n], thi[0][:, 0:n], thr[0][:, 0:n], ALU.mult, [thi[1], thr[1]], [thi[1]])
                tfp.put(thr)
                if samp:
                    a3 = av[0][:, 0:n].rearrange("p (s w) -> p s w", w=L)
                    b3 = thi[0][:, 0:n].rearrange("p (s w) -> p s w", w=L)
                    h0v = st0[:, c, 48:64].rearrange("p (s w) -> p s w", w=1)
                    tmpv = small[:, 16:32].rearrange("p (s w) -> p s w", w=1)
                    TT(tmpv, a3[:, :, 0:1], h0v, ALU.mult, [av[1], "st0"], ["small"])
                    TT(b3[:, :, 0:1], b3[:, :, 0:1], tmpv, ALU.add, [thi[1], "small"], [thi[1]])
                    TT(a3[:, :, 0:1], a3[:, :, 0:1],
                       cmask[:, 512:640].rearrange("p (s w) -> p s w", w=L)[:, :, 0:1],
                       ALU.mult, [av[1], "cmask"], [av[1]])
                hb = tfp.get()
                if samp or t == 0:
                    init, icell = 0.0, []
                else:
                    init, icell = carry[:, c, 3:4], ["carry%d" % c]
                P.add("dve", (lambda o, d0, d1, ini: (lambda e: e.tensor_tensor_scan(
                    out=o, data0=d0, data1=d1, initial=ini, op0=ALU.mult, op1=ALU.add)))(
                    hb[0][:, 0:n], av[0][:, 0:n], thi[0][:, 0:n], init),
                    [av[1], thi[1]] + icell, [hb[1]])
                if not samp and t < 3:
                    CP(carry[:, c, 3:4], hb[0][:, n - 1:n], [hb[1]], ["carry%d" % c])
                tfp.put(av)
                tfp.put(thi)
                yield
                if t == 3:
                    CP(stout[:, c, 3:4], hb[0][:, n - 1:n], [hb[1]], ["stout"], eng="pool")
                if samp:
                    CP(stout[:, c, 52:68].rearrange("p (s w) -> p s w", w=1),
                       hb[0][:, 0:n].rearrange("p (s w) -> p s w", w=L)[:, :, L - 1:L],
                       [hb[1]], ["stout"], eng="pool")
                STT(actT[:, c, lo:lo + n], gt[0][:, 0:n], 0.5, hb[0][:, 0:n], ALU.mult, ALU.mult,
                    [gt[1], hb[1]], [acell(c, t)])
                tfp.put(gt)
                tfp.put(hb)

            run_lanes([unit(ui) for ui in range(len(ulist))], 2, 4)
            for hf in range(2):
                bk = banks.get()
                TR([(bk[0][0:68, q * 128:(q + 1) * 128], stout[:, hf * 4 + q, :], identf[:]) for q in range(4)],
                   ["stout", "identf"], [bk[1]])
                tfb = tfp.get()
                ACP(tfb[0][0:68, 0:512], bk[0][0:68, 0:512], [bk[1]], [tfb[1]])
                banks.put(bk)
                cs_ = slice(hf * 512, (hf + 1) * 512)
                DMA("sp", o_cp[j, :, cs_], tfb[0][0:3, 0:512], [tfb[1]], ["o_cp"])
                DMA("sp", o_hp[j:j + 1, cs_], tfb[0][3:4, 0:512], [tfb[1]], ["o_hp"])
                DMA("sp", o_cs[j, :, cs_], tfb[0][4:52, 0:512], [tfb[1]], ["o_cs"])
                DMA("sp", o_hs[j, :, cs_], tfb[0][52:68, 0:512], [tfb[1]], ["o_hs"])
                tfp.put(tfb)
            out_proj(["mixout%d_0" % l, "mixout%d_1" % l], actT, ALLA)

        def gla_block(l):
            j = l // 2
            extra = [(st0[:].rearrange("p c n -> p (c n)").bitcast(BF16), "st0"),
                     (stout[:].rearrange("p c n -> p (c n)").bitcast(BF16)[:, 0:1024], "stout")]
            for x_ in extra:
                tbp.put(x_)
            norm_to_xn(PV_MIX + l * 8)
            slot = ws.next("glalo%d" % l)
            wl = slot[0][:, 0:128].rearrange("p (k n) -> p k n", n=16)
            for t in range(len(TILES)):
                lo, n = tcols(t)
                bk = banks.get()
                MM([(bk[0][0:16, 0:n], wl[:, k, :], xnT[:, k, lo:lo + n], k == 0, k == NCH - 1)
                    for k in range(NCH)], [slot[1]] + ALLN(t), [bk[1]])
                ACP(actT[0:16, 7, lo:lo + n], bk[0][0:16, 0:n], [bk[1]], [acell(7, t)])
                banks.put(bk)
            ws.release(slot)
            QS = 128.0 ** -0.5
            chain_done = {}
            gheld = {}
            if True:
                def gunit(t, h):
                    if t == 0:
                        gheld[("A", h)] = ws.next("glaA%d_%d" % (l, h))
                        MEMSET(Sf_p[:], 0.0, ["Sf_p"])
                        MEMSET(Sb_p[:], 0.0, ["Sb_p"])
                    sA = gheld[("A", h)]
                    wq_ = sA[0][:, 0:1024].rearrange("p (k n) -> p k n", n=128)
                    wk_ = sA[0][:, 1024:2048].rearrange("p (k n) -> p k n", n=128)
                    wv_ = sA[0][:, 2048:4096].rearrange("p (k n) -> p k n", n=256)
                    lo, n = tcols(t)
                    samp = (t == 4)
                    L = 8 if samp else 64
                    nch = n // L
                    nsub = n // 128
                    cm = cmask[:, 512:640] if samp else cmask[:, 0:n]
                    am = amask[:, 128:256] if samp else amask[:, 0:128]
                    bq = banks.get()
                    bk_ = banks.get()
                    bz = banks.get()
                    MM([(bq[0][:, 0:n], wq_[:, k, :], xnT[:, k, lo:lo + n], k == 0, k == NCH - 1)
                        for k in range(NCH)], [sA[1]] + ALLN(t), [bq[1]])
                    MM([(bk_[0][:, 0:n], wk_[:, k, :], xnT[:, k, lo:lo + n], k == 0, k == NCH - 1)
                        for k in range(NCH)], [sA[1]] + ALLN(t), [bk_[1]])
                    MM([(bz[0][:, 0:n], wa2[0:16, j, h * 128:(h + 1) * 128], actT[0:16, 7, lo:lo + n], True, True)],
                       ["wa2", acell(7, t)], [bz[1]])
                    yield
                    spl = tfp.get()
                    gb = dv[:, 64 + j * 4 + h:65 + j * 4 + h]
                    ACT(spl[0][:, 0:n], bz[0][:, 0:n], AF.Exp, [bz[1], "dv"], [spl[1]], scale=-1.0, bias=gb)
                    banks.put(bz)
                    ACT(spl[0][:, 0:n], spl[0][:, 0:n], AF.Ln, [spl[1], "cst"], [spl[1]], bias=c_one)
                    cs = tfp.get()
                    P.add("dve", (lambda o, d0, d1: (lambda e: e.tensor_tensor_scan(
                        out=o, data0=d0, data1=d1, initial=0.0, op0=ALU.mult, op1=ALU.add)))(
                        cs[0][:, 0:n], cm, spl[0][:, 0:n]), [spl[1], "cmask"], [cs[1]])
                    yield
                    ACT(spl[0][:, 0:n], cs[0][:, 0:n], AF.Exp, [cs[1]], [spl[1]], scale=-1.0 / 16.0)
                    qin = tbp.get()
                    STT(qin[0][:, 0:n], bq[0][:, 0:n], QS, spl[0][:, 0:n], ALU.mult, ALU.mult,
                        [bq[1], spl[1]], [qin[1]])
                    banks.put(bq)
                    yield
                    ACT(spl[0][:, 0:n], cs[0][:, 0:n], AF.Exp, [cs[1]], [spl[1]], scale=1.0 / 16.0)
                    TT(qin[0][:, 512:512 + n], bk_[0][:, 0:n], spl[0][:, 0:n], ALU.mult,
                       [bk_[1], spl[1]], [qin[1]])
                    banks.put(bk_)
                    tfp.put(spl)
                    eg = tfp.get()
                    cs3 = cs[0][:, 0:n].rearrange("p (s w) -> p s w", w=L)
                    ACT(eg[0][:, 0:nch].rearrange("p (s w) -> p s w", w=1), cs3[:, :, L - 1:L], AF.Exp,
                        [cs[1]], [eg[1]], scale=-1.0 / 16.0)
                    tfp.put(cs)
                    kend = tbp.get()
                    TT(kend[0][:, 0:n].rearrange("p (s w) -> p s w", w=L),
                       qin[0][:, 512:512 + n].rearrange("p (s w) -> p s w", w=L),
                       eg[0][:, 0:nch].rearrange("p (s w) -> p s w", w=1).to_broadcast([128, nch, L]),
                       ALU.mult, [qin[1], eg[1]], [kend[1]])
                    yield
                    vt = tbp.get()
                    for s0 in range(0, nsub, 2):
                        bv = banks.get()
                        ns2 = min(2, nsub - s0)
                        mms = []
                        for s in range(s0, s0 + ns2):
                            mms += [(bv[0][:, (s - s0) * 256:(s - s0 + 1) * 256],
                                     xnT[:, k, lo + s * 128:lo + (s + 1) * 128], wv_[:, k, :],
                                     k == 0, k == NCH - 1) for k in range(NCH)]
                        MM(mms, [sA[1]] + ALLN(t), [bv[1]])
                        ACP(vt[0][:, s0 * 256:(s0 + ns2) * 256], bv[0][:, 0:ns2 * 256], [bv[1]], [vt[1]])
                        banks.put(bv)
                        yield
                    if t == len(TILES) - 1:
                        ws.release(sA)
                    bt = banks.get()
                    btb = bt[0][:, 0:512].bitcast(BF16)
                    TR([(btb[:, s * 128:(s + 1) * 128], kend[0][:, s * 128:(s + 1) * 128], identb[:])
                        for s in range(nsub)], [kend[1], "identb"], [bt[1]])
                    CP(kend[0][:, 512:512 + n], btb[:, 0:n], [bt[1]], [kend[1]])
                    banks.put(bt)
                    yield
                    while 1 <= t <= 3 and not chain_done.get((h, t - 1), False):
                        yield
                    bo = [banks.get(), banks.get()]
                    for s in range(nsub):
                        ba = banks.get()
                        sc = slice(s * 128, (s + 1) * 128)
                        MM([(ba[0][:, 0:128], qin[0][:, 512 + s * 128:512 + (s + 1) * 128], qin[0][:, sc], True, True)],
                           [qin[1]], [ba[1]])
                        attm = tbp.get()
                        TT(attm[0][:, 0:128], ba[0][:, 0:128], am, ALU.mult, [ba[1], "amask"], [attm[1]])
                        banks.put(ba)
                        yield
                        MM([(bo[jv][0][:, sc], vt[0][:, s * 256 + jv * 128:s * 256 + (jv + 1) * 128],
                             attm[0][:, 0:128], True, False) for jv in range(2)],
                           [vt[1], attm[1]], [bo[0][1], bo[1][1]])
                        tbp.put(attm)
                        cps = nch // nsub
                        for ci in range(cps):
                            ch = s * cps + ci
                            col = slice(ch * L, (ch + 1) * L)
                            last = (ci == cps - 1)
                            if samp:
                                Sf = Sfs.get()
                                Sb = Sbs.get()
                                DMA("act", Sf[0][:], st_S[j, ch, h], [], [Sf[1]])
                                DMA("pool", Sb[0][:], st_S[j, ch, h], [], [Sb[1]])
                            else:
                                Sf = (Sf_p, "Sf_p")
                                Sb = (Sb_p, "Sb_p")
                            MM([(bo[jv][0][:, col], Sb[0][:, jv * 128:(jv + 1) * 128], qin[0][:, col], False, last)
                                for jv in range(2)], [Sb[1], qin[1]], [bo[0][1], bo[1][1]])
                            bs = banks.get()
                            if samp:
                                kem = tbp.get()
                                TS(kem[0][:, 0:128], kend[0][:, 512:640], chs[:, ch:ch + 1], 0.0, ALU.mult, ALU.add,
                                   [kend[1], "chs"], [kem[1]])
                                MM([(bs[0][:, 0:256], kem[0][:, 0:128], vt[0][:, 0:256], True, True)],
                                   [kem[1], vt[1]], [bs[1]])
                                tbp.put(kem)
                            else:
                                r0 = (ch % 2) * 64
                                MM([(bs[0][:, 0:256], kend[0][r0:r0 + 64, 512 + s * 128:512 + (s + 1) * 128],
                                     vt[0][r0:r0 + 64, s * 256:(s + 1) * 256], True, True)],
                                   [kend[1], vt[1]], [bs[1]])
                            if not samp:
                                STT(Sb[0][:], Sf[0][:], eg[0][:, ch:ch + 1], bs[0][:, 0:256], ALU.mult, ALU.add,
                                    [Sf[1], eg[1], bs[1]], [Sb[1]])
                            STT(Sf[0][:], Sf[0][:], eg[0][:, ch:ch + 1], bs[0][:, 0:256], ALU.mult, ALU.add,
                                [Sf[1], eg[1], bs[1]], [Sf[1]])
                            banks.put(bs)
                            yield
                            if samp:
                                DMA("sp", o_Ss[j, ch, h], Sf[0][:], [Sf[1]], ["o_Ss"])
                                Sfs.put(Sf)
                                Sbs.put(Sb)
                    chain_done[(h, t)] = True
                    tbp.put(qin)
                    tbp.put(kend)
                    tbp.put(vt)
                    tfp.put(eg)
                    if t == 3:
                        DMA("sp", o_Sp[j, h], Sf_p[:], ["Sf_p"], ["o_Sp"])
                    bm = banks.get()
                    for jv in range(2):
                        sq = tbp.get()
                        ACT(sq[0][:, 0:n], bo[jv][0][:, 0:n], AF.Square, [bo[jv][1]], [sq[1]])
                        MM([(bm[0][:, 0:n], ones256, sq[0][:, 0:n], jv == 0, jv == 1)], [sq[1], "ones_m"], [bm[1]])
                        tbp.put(sq)
                    rs = tfp.get()
                    ACT(rs[0][:, 0:n], bm[0][:, 0:n], AF.Ln, [bm[1], "cst"], [rs[1]], bias=c_eps)
                    banks.put(bm)
                    ACT(rs[0][:, 0:n], rs[0][:, 0:n], AF.Exp, [rs[1]], [rs[1]], scale=-0.5)
                    yield
                    if ("B", h) not in gheld:
                        gheld[("B", h)] = ws.next("glaB%d_%d" % (l, h))
                    sB = gheld[("B", h)]
                    wg_ = sB[0][:, 0:2048].rearrange("p (k n) -> p k n", n=256)
                    for jv in range(2):
                        bg = banks.get()
                        MM([(bg[0][:, 0:n], wg_[:, k, jv * 128:(jv + 1) * 128], xnT[:, k, lo:lo + n],
                             k == 0, k == NCH - 1) for k in range(NCH)], [sB[1]] + ALLN(t), [bg[1]])
                        sg = tfp.get()
                        ACT(sg[0][:, 0:n], bg[0][:, 0:n], AF.Tanh, [bg[1]], [sg[1]], scale=0.5)
                        STT(sg[0][:, 0:n], sg[0][:, 0:n], 1.0, bg[0][:, 0:n], ALU.add, ALU.mult,
                            [sg[1], bg[1]], [sg[1]])
                        banks.put(bg)
                        yield
                        on = tfp.get()
                        gcol = PV_GNG + j * 2 + jv
                        STT(on[0][:, 0:n], bo[jv][0][:, 0:n], pv[:, gcol:gcol + 1], rs[0][:, 0:n],
                            ALU.mult, ALU.mult, [bo[jv][1], rs[1], "pv"], [on[1]])
                        STT(actT[:, 2 * h + jv, lo:lo + n], on[0][:, 0:n], 0.5, sg[0][:, 0:n], ALU.mult, ALU.mult,
                            [on[1], sg[1]], [acell(2 * h + jv, t)])
                        tfp.put(on)
                        tfp.put(sg)
                    tfp.put(rs)
                    banks.put(bo[0])
                    banks.put(bo[1])
                    if t == len(TILES) - 1:
                        ws.release(sB)
                run_lanes([gunit(t, h) for h in range(4) for t in range(len(TILES))], 2, 12)
            for x_ in extra:
                tbp.free.remove(x_)
            out_proj(["mixout%d_0" % l, "mixout%d_1" % l], actT, ALLA)

        def make_kT(slot):
            kT = [tbp.get(), tbp.get()]
            for half in range(2):
                for mt in range(2):
                    bt = banks.get()
                    btb = bt[0][:, 0:512].bitcast(BF16)
                    TR([(btb[:, q * 128:(q + 1) * 128],
                         slot[0][:, mt * 1024 + (half * 4 + q) * 128:mt * 1024 + (half * 4 + q + 1) * 128],
                         identb[:]) for q in range(4)], [slot[1], "identb"], [bt[1]])
                    if mt == 0:
                        CP(kT[half][0][:, 0:1024].rearrange("p (q m) -> p q m", m=256)[:, :, mt * 128:(mt + 1) * 128],
                           btb[:, 0:512].rearrange("p (q m) -> p q m", m=128), [bt[1]], [kT[half][1]])
                    else:
                        ACP(kT[half][0][:, 0:1024].rearrange("p (q m) -> p q m", m=256)[:, :, mt * 128:(mt + 1) * 128],
                            btb[:, 0:512].rearrange("p (q m) -> p q m", m=128), [bt[1]], [kT[half][1]])
                    banks.put(bt)
            return kT

        def attention(h, kT, slot, qfn, qcells, n, out_ap_fn, out_cells_fn):
            Vv = slot[0][:, 2048:4096].rearrange("p (b n) -> p b n", n=1024)
            bsc = [banks.get(), banks.get()]
            pT = tbp.get()
            for mt in range(2):
                mms = []
                for dc in range(2):
                    kc = 2 * h + dc
                    ksrc = kT[kc // 4][0][:, (kc % 4) * 256 + mt * 128:(kc % 4) * 256 + (mt + 1) * 128]
                    mms.append((bsc[mt][0][:, 0:n], ksrc, qfn(dc), dc == 0, dc == 1))
                MM(mms, [kT[0][1], kT[1][1]] + qcells, [bsc[mt][1]])
                ACT(pT[0][:, mt * 512:mt * 512 + n], bsc[mt][0][:, 0:n], AF.Exp, [bsc[mt][1]], [pT[1]],
                    scale=1.0 / 16.0)
                banks.put(bsc[mt])
            bd = banks.get()
            MM([(bd[0][:, 0:n], ones1, pT[0][:, mt * 512:mt * 512 + n], mt == 0, mt == 1) for mt in range(2)],
               [pT[1], "ones_m"], [bd[1]])
            rec = tfp.get()
            P.add("dve", (lambda o, i: (lambda e: e.reciprocal(out=o, in_=i)))(rec[0][:, 0:n], bd[0][:, 0:n]),
                  [bd[1]], [rec[1]])
            banks.put(bd)
            for jv in range(2):
                bp = banks.get()
                MM([(bp[0][:, 0:n], Vv[:, mt, (2 * h + jv) * 128:(2 * h + jv + 1) * 128],
                     pT[0][:, mt * 512:mt * 512 + n], mt == 0, mt == 1) for mt in range(2)],
                   [slot[1], pT[1]], [bp[1]])
                TT(out_ap_fn(jv), bp[0][:, 0:n], rec[0][:, 0:n], ALU.mult, [bp[1], rec[1]], out_cells_fn(jv))
                banks.put(bp)
            tbp.put(pT)
            tfp.put(rec)

        def attention_sample(sq_, kT, slot, qs, l):
            Vv = slot[0][:, 2048:4096].rearrange("p (b n) -> p b n", n=1024)
            bsc = banks.get()
            mms = []
            for mt in range(2):
                for h in range(4):
                    for dc in range(2):
                        kc = 2 * h + dc
                        ksrc = kT[kc // 4][0][:, (kc % 4) * 256 + mt * 128:(kc % 4) * 256 + (mt + 1) * 128]
                        mms.append((bsc[0][:, (mt * 4 + h) * 8:(mt * 4 + h + 1) * 8], ksrc,
                                    qs[0][:, kc * 128 + sq_ * 8:kc * 128 + (sq_ + 1) * 8], dc == 0, dc == 1))
            MM(mms, [kT[0][1], kT[1][1], qs[1]], [bsc[1]])
            pT = tbp.get()
            ACT(pT[0][:, 0:64], bsc[0][:, 0:64], AF.Exp, [bsc[1]], [pT[1]], scale=1.0 / 16.0)
            banks.put(bsc)
            bd = banks.get()
            MM([(bd[0][:, 0:32], ones1, pT[0][:, mt * 32:(mt + 1) * 32], mt == 0, mt == 1) for mt in range(2)],
               [pT[1], "ones_m"], [bd[1]])
            bp = banks.get()
            mms = []
            for h in range(4):
                for jv in range(2):
                    for mt in range(2):
                        mms.append((bp[0][:, (h * 2 + jv) * 8:(h * 2 + jv + 1) * 8],
                                    Vv[:, mt, (2 * h + jv) * 128:(2 * h + jv + 1) * 128],
                                    pT[0][:, (mt * 4 + h) * 8:(mt * 4 + h + 1) * 8], mt == 0, mt == 1))
            MM(mms, [slot[1], pT[1]], [bp[1]])
            tbp.put(pT)
            rec = tfp.get()
            P.add("dve", (lambda o, i: (lambda e: e.reciprocal(out=o, in_=i)))(rec[0][:, 0:32], bd[0][:, 0:32]),
                  [bd[1]], [rec[1]])
            banks.put(bd)
            col = slice(2048 + sq_ * 8, 2048 + (sq_ + 1) * 8)
            pv4 = bp[0][:, 0:64].rearrange("p (h j t) -> p h j t", j=2, t=8)
            for jv in range(2):
                TT(actT[:, :, col].rearrange("p (h j) t -> p h j t", j=2)[:, :, jv, :], pv4[:, :, jv, :],
                   rec[0][:, 0:32].rearrange("p (h t) -> p h t", t=8),
                   ALU.mult, [bp[1], rec[1]], [acell(2 * h + jv, 4) for h in range(4)])
            banks.put(bp)
            tfp.put(rec)

        def xa_block(l):
            extra = [(st0[:].rearrange("p c n -> p (c n)").bitcast(BF16), "st0"),
                     (stout[:].rearrange("p c n -> p (c n)").bitcast(BF16)[:, 0:1024], "stout")]
            for x_ in extra:
                tbp.put(x_)
            norm_to_xn(PV_XA + l * 8)
            kvslot = ws.next("kvp%d" % l)
            kT = make_kT(kvslot)
            qs = tbp.get()
            Vv = kvslot[0][:, 2048:4096].rearrange("p (b n) -> p b n", n=1024)
            held = {}
            units = [(h, t) for h in range(4) for t in range(len(TILES))]

            def s1(h, t):
                if t == 0:
                    held[h] = ws.next("wq%d_%d" % (l, h))
                slot = held[h]
                wq_ = slot[0][:, 0:2048].rearrange("p (k n) -> p k n", n=256)
                lo, n = tcols(t)
                qt = tbp.get() if t < 4 else None
                for dc in range(2):
                    bq = banks.get()
                    MM([(bq[0][:, 0:n], wq_[:, k, dc * 128:(dc + 1) * 128], xnT[:, k, lo:lo + n],
                         k == 0, k == NCH - 1) for k in range(NCH)], [slot[1]] + ALLN(t), [bq[1]])
                    if t < 4:
                        ACP(qt[0][:, dc * 512:dc * 512 + n], bq[0][:, 0:n], [bq[1]], [qt[1]])
                    else:
                        ACP(qs[0][:, (2 * h + dc) * 128:(2 * h + dc + 1) * 128], bq[0][:, 0:n], [bq[1]], [qs[1]])
                    banks.put(bq)
                if t == len(TILES) - 1:
                    ws.release(slot)
                    del held[h]
                return qt

            def s2(h, t, qt):
                lo, n = tcols(t)
                pT = tbp.get()
                for mt in range(2):
                    bsc = banks.get()
                    mms = []
                    for dc in range(2):
                        kc = 2 * h + dc
                        ksrc = kT[kc // 4][0][:, (kc % 4) * 256 + mt * 128:(kc % 4) * 256 + (mt + 1) * 128]
                        mms.append((bsc[0][:, 0:n], ksrc, qt[0][:, dc * 512:dc * 512 + n], dc == 0, dc == 1))
                    MM(mms, [kT[0][1], kT[1][1], qt[1]], [bsc[1]])
                    ACT(pT[0][:, mt * 512:mt * 512 + n], bsc[0][:, 0:n], AF.Exp, [bsc[1]], [pT[1]],
                        scale=1.0 / 16.0)
                    banks.put(bsc)
                tbp.put(qt)
                return pT

            def s3(h, t, pT):
                lo, n = tcols(t)
                bd = banks.get()
                MM([(bd[0][:, 0:n], ones1, pT[0][:, mt * 512:mt * 512 + n], mt == 0, mt == 1) for mt in range(2)],
                   [pT[1], "ones_m"], [bd[1]])
                rec = tfp.get()
                P.add("dve", (lambda o, i: (lambda e: e.reciprocal(out=o, in_=i)))(rec[0][:, 0:n], bd[0][:, 0:n]),
                      [bd[1]], [rec[1]])
                banks.put(bd)
                for jv in range(2):
                    bp = banks.get()
                    MM([(bp[0][:, 0:n], Vv[:, mt, (2 * h + jv) * 128:(2 * h + jv + 1) * 128],
                         pT[0][:, mt * 512:mt * 512 + n], mt == 0, mt == 1) for mt in range(2)],
                       [kvslot[1], pT[1]], [bp[1]])
                    TT(actT[:, 2 * h + jv, lo:lo + n], bp[0][:, 0:n], rec[0][:, 0:n], ALU.mult,
                       [bp[1], rec[1]], [acell(2 * h + jv, t)])
                    banks.put(bp)
                tbp.put(pT)
                tfp.put(rec)

            q1 = []
            q2 = []
            for (h, t) in units:
                qt = s1(h, t)
                if q2:
                    s3(*q2.pop(0))
                if q1:
                    hh, tt, qq = q1.pop(0)
                    q2.append((hh, tt, s2(hh, tt, qq)))
                if t < 4:
                    q1.append((h, t, qt))
            while q1 or q2:
                if q2:
                    s3(*q2.pop(0))
                if q1:
                    hh, tt, qq = q1.pop(0)
                    q2.append((hh, tt, s2(hh, tt, qq)))
            ws.release(kvslot)
            tbp.put(kT[0])
            tbp.put(kT[1])
            def sample_iter():
                pend = None
                for s in range(NSEQ_S + 1):
                    cur = None
                    if s < NSEQ_S:
                        kvslot = ws.next("kvs%d_%d" % (l, s))
                        cur = (s, make_kT(kvslot), kvslot)
                    if pend is not None:
                        ps_, pk_, pslot_ = pend
                        attention_sample(ps_, pk_, pslot_, qs, l)
                        ws.release(pslot_)
                        tbp.put(pk_[0])
                        tbp.put(pk_[1])
                    pend = cur
                    yield

            def oproj(slot, hf, tiles, hook):
                wv_ = slot[0][:, 0:4096].rearrange("p (k n) -> p k n", n=512)
                step = 0
                for oc4 in range(4):
                    oc = hf * 4 + oc4
                    for t in tiles:
                        lo, n = tcols(t)
                        bk = banks.get()
                        MM([(bk[0][:, 0:n], wv_[:, k, oc4 * 128:(oc4 + 1) * 128], actT[:, k, lo:lo + n],
                             k == 0, k == NCH - 1) for k in range(NCH)], [slot[1]] + ALLA(t), [bk[1]])
                        TT(xT[:, oc, lo:lo + n], xT[:, oc, lo:lo + n], bk[0][:, 0:n], ALU.add,
                           [bk[1], xcell(oc, t)], [xcell(oc, t)])
                        banks.put(bk)
                        step += 1
                        if hook is not None and step % 2 == 0:
                            hook()

            it = sample_iter()
            next(it)
            next(it)
            so0 = ws.next("wo%d_0" % l)
            oproj(so0, 0, range(4), lambda: next(it, None))
            ws.release(so0)
            so1 = ws.next("wo%d_1" % l)
            oproj(so1, 1, range(4), lambda: next(it, None))
            for _ in it:
                pass
            oproj(so1, 1, [4], None)
            ws.release(so1)
            so0 = ws.next("wo%d_0b" % l)
            oproj(so0, 0, [4], None)
            ws.release(so0)
            tbp.put(qs)
            for x_ in extra:
                tbp.free.remove(x_)

        def mlp_block(l):
            norm_to_xn(PV_MLP + l * 8)
            units = [(g, t) for g in range(8) for t in range(len(TILES))]
            held = {}

            def h_phase(g, t):
                if t == 0:
                    s1 = ws.next("w1_%d_%d" % (l, g))
                    s2 = ws.next("w2_%d_%d" % (l, g))
                    held[g] = (s1, s2)
                s1, s2 = held[g]
                w1v = s1[0][:, 0:4096].rearrange("p (k n) -> p k n", n=512)
                lo, n = tcols(t)
                hs = [tbp.get(), tbp.get()]
                for kk in range(4):
                    bh = banks.get()
                    MM([(bh[0][:, 0:n], w1v[:, k, kk * 128:(kk + 1) * 128], xnT[:, k, lo:lo + n],
                         k == 0, k == NCH - 1) for k in range(NCH)], [s1[1]] + ALLN(t), [bh[1]])
                    hv = hs[kk // 2][0][:, (kk % 2) * 512:(kk % 2) * 512 + n]
                    ACT(hv, bh[0][:, 0:n], AF.Relu, [bh[1]], [hs[kk // 2][1]])
                    banks.put(bh)
                    TT(hv, hv, hv, ALU.mult, [hs[kk // 2][1]], [hs[kk // 2][1]])
                return hs

            def y_phase(g, t, hs):
                s1, s2 = held[g]
                w2v = s2[0][:, 0:4096].rearrange("p (k n) -> p k n", n=1024)
                lo, n = tcols(t)
                for oc in range(NCH):
                    by = banks.get()
                    MM([(by[0][:, 0:n], w2v[:, kk, oc * 128:(oc + 1) * 128],
                         hs[kk // 2][0][:, (kk % 2) * 512:(kk % 2) * 512 + n], kk == 0, kk == 3)
                        for kk in range(4)], [s2[1], hs[0][1], hs[1][1]], [by[1]])
                    TT(xT[:, oc, lo:lo + n], xT[:, oc, lo:lo + n], by[0][:, 0:n], ALU.add,
                       [by[1], xcell(oc, t)], [xcell(oc, t)])
                    banks.put(by)
                tbp.put(hs[0])
                tbp.put(hs[1])
                if t == len(TILES) - 1:
                    ws.release(s1)
                    ws.release(s2)
                    del held[g]

            prev = None
            for (g, t) in units:
                if t == 0 and prev is not None:
                    y_phase(*prev)
                    prev = None
                hs = h_phase(g, t)
                if prev is not None:
                    y_phase(*prev)
                prev = (g, t, hs)
            y_phase(*prev)

        for l in range(DEPTH):
            if enabled("mix", l):
                if l % 2 == 0:
                    rg_block(l)
                else:
                    gla_block(l)
            if enabled("xa", l):
                use_slot_x(True)
                xa_block(l)
                use_slot_x(False)
            if enabled("mlp", l):
                mlp_block(l)

        for t in range(len(TILES)):
            lo, n = tcols(t)
            rs = norm_rstd(t)
            for s in range(n // 128):
                tok0 = lo + s * 128
                for hf in range(2):
                    yt = tfp.get()
                    for q in range(4):
                        c = hf * 4 + q
                        STT(yt[0][:, q * 128:(q + 1) * 128], xT[:, c, tok0:tok0 + 128],
                            pv[:, PV_FIN + c:PV_FIN + c + 1], rs[0][:, s * 128:(s + 1) * 128],
                            ALU.mult, ALU.mult, [xcell(c, t), rs[1], "pv"], [yt[1]])
                    bk = banks.get()
                    TR([(bk[0][:, q * 128:(q + 1) * 128], yt[0][:, q * 128:(q + 1) * 128], identf[:])
                        for q in range(4)], [yt[1], "identf"], [bk[1]])
                    tfp.put(yt)
                    yo = tfp.get()
                    ACP(yo[0][:, 0:512], bk[0][:, 0:512], [bk[1]], [yo[1]])
                    banks.put(bk)
                    if t < 4:
                        DMA("sp", o_yp[tok0:tok0 + 128, hf * 512:(hf + 1) * 512], yo[0][:, 0:512], [yo[1]], ["o_y"])
                    else:
                        DMA("sp", o_ys[:, hf * 512:(hf + 1) * 512], yo[0][:, 0:512], [yo[1]], ["o_y"])
                    tfp.put(yo)
            tfp.put(rs)

        assert ws.consumed == len(ws.plan), (ws.consumed, len(ws.plan))
        import os as _os
        if _os.environ.get("MK_VERBOSE"):
            print("sbuf bytes remaining", nc.sbuf_bytes_remaining, "ops", len(P.ops))
        P.emit(st)
    return nc


def _consts():
    c = np.zeros((128, 1168), np.float32)
    c[:, 0:128] = np.eye(128, dtype=np.float32)
    s = np.arange(128)[:, None]
    t = np.arange(128)[None, :]
    c[:, 128:256] = ((s <= t) & (s // 64 == t // 64)).astype(np.float32)
    c[:, 256:384] = ((s <= t) & (s // 8 == t // 8)).astype(np.float32)
    cm = np.ones((128, 640), np.float32)
    cm[:, 0:512:64] = 0.0
    cm[:, 512:640:8] = 0.0
    c[:, 384:1024] = cm
    c[:, 1024:1040] = (np.arange(128)[:, None] // 8 == np.arange(16)[None, :]).astype(np.float32)
    return c


def _pvec(inp):
    def fm(v):
        v = np.asarray(v, np.float32)
        lead = v.shape[:-1]
        n = v.shape[-1] // 128
        v = v.reshape(lead + (n, 128))
        v = np.moveaxis(v, -1, 0)
        return v.reshape(128, -1)
    p = np.zeros((128, 320), np.float32)
    p[:, 0:32] = fm(inp["norm_mix_g"])
    p[:, 32:64] = fm(inp["norm_xa_g"])
    p[:, 64:96] = fm(inp["norm_mem_g"])
    p[:, 96:128] = fm(inp["norm_mlp_g"])
    p[:, 128:136] = fm(inp["final_norm_g"])
    cw = np.asarray(inp["rg_conv_w"], np.float32)
    cw = cw.reshape(2, 4, 8, 128).transpose(3, 0, 2, 1)
    p[:, 136:200] = cw.reshape(128, 64)
    p[:, 200:216] = fm(inp["rg_conv_b"])
    p[:, 216:232] = fm(inp["rg_b_a"])
    p[:, 232:248] = fm(inp["rg_b_x"])
    p[:, 248:264] = fm(inp["rg_lambda"])
    p[:, 264:272] = fm(inp["gla_b_a"])
    p[:, 272:276] = fm(inp["gla_norm_g"])
    return p


_NC_CACHE = {}


def kernel(**inputs):
    inp = {k: np.asarray(v) for k, v in inputs.items()}
    import os
    stage = int(os.environ.get("MK_STAGE", "99"))
    ncores = int(os.environ.get("MK_CORES", str(NCORES)))
    if "nc" not in _NC_CACHE:
        _NC_CACHE["nc"] = build_program(stage)
    nc = _NC_CACHE["nc"]
    consts = _consts()
    pvec = _pvec(inp)
    shared = {
        "pvec": pvec, "consts": consts,
        "rg_w_in": inp["rg_w_in"], "rg_w_a": inp["rg_w_a"], "rg_w_x": inp["rg_w_x"],
        "rg_w_out": inp["rg_w_out"], "gla_w_in": inp["gla_w_in"], "gla_w_a2": inp["gla_w_a2"],
        "gla_w_out": inp["gla_w_out"], "xa_wq": inp["xa_wq"], "xa_wk": inp["xa_wk"],
        "xa_wv": inp["xa_wv"], "xa_wo": inp["xa_wo"], "mlp_w1": inp["mlp_w1"], "mlp_w2": inp["mlp_w2"],
    }
    in_maps = []
    for b in range(NCORES):
        sl = slice(b * 16, (b + 1) * 16)
        m = dict(shared)
        m["xp"] = np.ascontiguousarray(inp["x_prompt"][b])
        m["xs"] = np.ascontiguousarray(inp["x_sample"][sl].reshape(128, D))
        m["mem"] = np.ascontiguousarray(inp["mem_prompt"][b])
        m["st_h"] = np.ascontiguousarray(inp["state_rglru_h"][:, sl])
        m["st_conv"] = np.ascontiguousarray(inp["state_rglru_conv"][:, sl].reshape(2, 48, D))
        m["st_S"] = np.ascontiguousarray(inp["state_gla_S"][:, sl])
        m["ck"] = np.ascontiguousarray(inp["cache_mem_k"][:, sl].reshape(DEPTH, 16, 256, D))
        m["cv"] = np.ascontiguousarray(inp["cache_mem_v"][:, sl].reshape(DEPTH, 16, 256, D))
        in_maps.append(m)
    res = run_bass_kernel_spmd(nc, in_maps[:ncores], core_ids=list(range(ncores)))
    R = list(res.results)
    while len(R) < NCORES:
        R.append(R[0])
    f = np.float32
    y_prompt = np.stack([R[b]["o_yp"] for b in range(NCORES)]).astype(f)
    y_sample = np.concatenate([R[b]["o_ys"].reshape(16, 8, D) for b in range(NCORES)], 0).astype(f)
    mem_k = np.stack([R[b]["o_mk"].reshape(DEPTH, 256, 4, 256) for b in range(NCORES)], 1).astype(f)
    mem_v = np.stack([R[b]["o_mv"].reshape(DEPTH, 256, 4, 256) for b in range(NCORES)], 1).astype(f)
    h_p = np.stack([R[b]["o_hp"] for b in range(NCORES)], 1).astype(f)
    c_p = np.stack([R[b]["o_cp"] for b in range(NCORES)], 1).astype(f)
    S_p = np.stack([R[b]["o_Sp"] for b in range(NCORES)], 1).astype(f)
    h_s = np.concatenate([R[b]["o_hs"] for b in range(NCORES)], 1).astype(f)
    c_s = np.concatenate([R[b]["o_cs"].reshape(2, 16, 3, D) for b in range(NCORES)], 1).astype(f)
    S_s = np.concatenate([R[b]["o_Ss"] for b in range(NCORES)], 1).astype(f)
    return (y_prompt, y_sample, mem_k, mem_v, h_p, c_p, S_p, h_s, c_s, S_s)
```
